# Optimizing a Trainium2 kernel written in Bass

```python
import jax, jax.numpy as jnp
from jax import lax
import numpy as np

D_MODEL = 1024
BATCH = 32
SEQ = 256
DEPTH = 2
DEC_BATCH = 2
DEC_SEQ = 1024
PAST_LEN = 256

GRID_W = 64
N_MIXERS = 2
N_ATTN_LAYERS = (DEPTH + 1) // 2
N_REC_LAYERS = DEPTH // 2
HEAD_DIM = 64
N_Q_HEADS = D_MODEL // HEAD_DIM
N_KV_HEADS = N_Q_HEADS // 4
GQA_GROUP = N_Q_HEADS // N_KV_HEADS
QKV_DIM = (N_Q_HEADS + 2 * N_KV_HEADS) * HEAD_DIM
WINDOW = 128
BLOCK = 128
ROPE_BASE = 10000.0
REC_EXPAND = 128
REC_HEADS = D_MODEL // REC_EXPAND
REC_DK = REC_EXPAND
REC_DV = D_MODEL // REC_HEADS
REC_IN_DIM = 3 * REC_HEADS * REC_DK + 2 * REC_HEADS * REC_DV
CHUNK = 64
D_FF = 2816
MACARON_WEIGHT = 0.5
EPS = 1e-6
MASK_VALUE = -1e30

kernel_name = 'hybrid_diffusion_swa_hgrn2_macaron_step'


def rmsnorm(x, g):
    xf = x.astype(jnp.float32)
    y = xf * lax.rsqrt(jnp.mean(xf * xf, axis=-1, keepdims=True) + EPS)
    return (y * g.astype(jnp.float32)).astype(x.dtype)


def sublayer_in(x, g, mod, slot):
    return rmsnorm(x, g) * (1 + mod[..., slot, 1, :]) + mod[..., slot, 0, :]


def sublayer_out(x, out, g, mod, slot, weight):
    return x + weight * mod[..., slot, 2, :] * rmsnorm(out, g)


def swiglu(h, w_in, w_out):
    a, b = jnp.split(h @ w_in, 2, axis=-1)
    return (jax.nn.silu(a) * b) @ w_out


def ffn_sublayer(x, g_pre, g_post, mod, slot, w_in, w_out):
    h = sublayer_in(x, g_pre, mod, slot)
    return sublayer_out(x, swiglu(h, w_in, w_out), g_post, mod, slot, MACARON_WEIGHT)


def attn_project(h, w_qkv):
    B, L, _ = h.shape
    q, k, v = jnp.split(h @ w_qkv, [N_Q_HEADS * HEAD_DIM, (N_Q_HEADS + N_KV_HEADS) * HEAD_DIM], axis=-1)
    return (q.reshape(B, L, N_KV_HEADS, GQA_GROUP, HEAD_DIM),
            k.reshape(B, L, N_KV_HEADS, HEAD_DIM),
            v.reshape(B, L, N_KV_HEADS, HEAD_DIM))


def axial_rope(x, T):
    rows = T // GRID_W
    row = jnp.repeat(jnp.arange(rows), GRID_W).astype(jnp.float32)
    col = jnp.tile(jnp.arange(GRID_W), rows).astype(jnp.float32)
    nf = HEAD_DIM // 4
    inv = ROPE_BASE ** (-jnp.arange(nf, dtype=jnp.float32) / nf)
    bshape = (1, T) + (1,) * (x.ndim - 3) + (nf,)

    def rot(xh, pos):
        ang = (pos[:, None] * inv[None, :]).reshape(bshape)
        cos = jnp.cos(ang).astype(x.dtype)
        sin = jnp.sin(ang).astype(x.dtype)
        x1, x2 = jnp.split(xh, 2, axis=-1)
        return jnp.concatenate([x1 * cos - x2 * sin, x1 * sin + x2 * cos], axis=-1)

    half = HEAD_DIM // 2
    return jnp.concatenate([rot(x[..., :half], row), rot(x[..., half:], col)], axis=-1)


def softmax_with_sink(s, sink):
    sk = sink.astype(jnp.float32).reshape(1, N_KV_HEADS, GQA_GROUP, 1, 1)
    m = jnp.maximum(jnp.max(s, axis=-1, keepdims=True), sk)
    e = jnp.exp(s - m)
    return e / (jnp.sum(e, axis=-1, keepdims=True) + jnp.exp(sk - m))


def ctx_attention(q, k, v, sink):
    B, L = q.shape[:2]
    scale = HEAD_DIM ** -0.5

    def one(b):
        qb = lax.dynamic_slice_in_dim(q, b * BLOCK, BLOCK, axis=1)
        s = jnp.einsum('bqhgd,bkhd->bhgqk', qb, k).astype(jnp.float32) * scale
        p = softmax_with_sink(s, sink).astype(v.dtype)
        return jnp.einsum('bhgqk,bkhd->bqhgd', p, v)

    o = lax.map(one, jnp.arange(L // BLOCK))
    return jnp.moveaxis(o, 0, 1).reshape(B, L, N_Q_HEADS * HEAD_DIM)


def latent_attention(q, k, v, k_ctx, v_ctx, sink):
    B, T = q.shape[:2]
    scale = HEAD_DIM ** -0.5
    pad = ((0, 0), (BLOCK, BLOCK), (0, 0), (0, 0))
    k_pad = jnp.pad(k, pad)
    v_pad = jnp.pad(v, pad)

    def one(b):
        start = b * BLOCK
        qb = lax.dynamic_slice_in_dim(q, start, BLOCK, axis=1)
        kw = lax.dynamic_slice_in_dim(k_pad, start, 3 * BLOCK, axis=1)
        vw = lax.dynamic_slice_in_dim(v_pad, start, 3 * BLOCK, axis=1)
        qi = start + jnp.arange(BLOCK)
        kj = start - BLOCK + jnp.arange(3 * BLOCK)
        valid = (jnp.abs(qi[:, None] - kj[None, :]) <= WINDOW) & (kj >= 0)[None, :] & (kj < T)[None, :]
        s_w = jnp.einsum('bqhgd,bkhd->bhgqk', qb, kw).astype(jnp.float32) * scale
        s_w = jnp.where(valid, s_w, MASK_VALUE)
        s_c = jnp.einsum('bqhgd,bkhd->bhgqk', qb, k_ctx).astype(jnp.float32) * scale
        p = softmax_with_sink(jnp.concatenate([s_w, s_c], axis=-1), sink).astype(v.dtype)
        return (jnp.einsum('bhgqk,bkhd->bqhgd', p[..., :3 * BLOCK], vw)
                + jnp.einsum('bhgqk,bkhd->bqhgd', p[..., 3 * BLOCK:], v_ctx))

    o = lax.map(one, jnp.arange(T // BLOCK))
    return jnp.moveaxis(o, 0, 1).reshape(B, T, N_Q_HEADS * HEAD_DIM)


def hgrn_gates(z, lb):
    zf = z.astype(jnp.float32)
    lbf = lb.reshape(REC_HEADS, REC_DK)
    logf = jnp.log(lbf + (1 - lbf) * jax.nn.sigmoid(zf))
    key = (1 - lbf) * jax.nn.sigmoid(-zf)
    return logf, key


def chunk_recurrence(q, k, v, logf, s0):
    B, T, H, _ = q.shape
    nc = T // CHUNK
    rs = lambda t: t.reshape(B, nc, CHUNK, H, t.shape[-1])
    q, k, v, logf = rs(q), rs(k), rs(v), rs(logf)
    bcum = jnp.cumsum(logf, axis=2)
    blast = bcum[:, :, -1]
    q_dec = q * jnp.exp(bcum)
    k_inv = k * jnp.exp(-bcum)
    causal = jnp.tril(jnp.ones((CHUNK, CHUNK), dtype=bool))
    a = jnp.where(causal, jnp.einsum('bnchk,bnshk->bnhcs', q_dec, k_inv), 0.0)
    o_intra = jnp.einsum('bnhcs,bnshv->bnchv', a, v)
    k_end = k * jnp.exp(blast[:, :, None] - bcum)
    upd = jnp.einsum('bnshk,bnshv->bnhkv', k_end, v)
    dec = jnp.exp(blast)

    def step(s, xs):
        d, u = xs
        return d[..., None] * s + u, s

    s_fin, s_start = lax.scan(step, s0, (jnp.moveaxis(dec, 1, 0), jnp.moveaxis(upd, 1, 0)))
    s_start = jnp.moveaxis(s_start, 0, 1)
    o_inter = jnp.einsum('bnchk,bnhkv->bnchv', q_dec, s_start)
    return (o_intra + o_inter).reshape(B, T, H, v.shape[-1]), s_fin


def hgrn_mixer(h, s0, w_in, lb_f, lb_b, g_norm, w_out):
    B, T, _ = h.shape
    q, i_in, z_f, z_b, g = jnp.split(h @ w_in, [REC_HEADS * REC_DK, REC_HEADS * (REC_DK + REC_DV),
                                              REC_HEADS * (2 * REC_DK + REC_DV), REC_HEADS * (3 * REC_DK + REC_DV)], axis=-1)
    shk = (B, T, REC_HEADS, REC_DK)
    qf = jax.nn.silu(q.reshape(shk).astype(jnp.float32)) * (REC_DK ** -0.5)
    vf = i_in.reshape(B, T, REC_HEADS, REC_DV).astype(jnp.float32)
    logf_f, k_f = hgrn_gates(z_f.reshape(shk), lb_f)
    logf_b, k_b = hgrn_gates(z_b.reshape(shk), lb_b)
    s0f = s0.astype(jnp.float32)
    o_f, s_f = chunk_recurrence(qf, k_f, vf, logf_f, s0f[:, 0])
    flip = lambda t: jnp.flip(t, axis=1)
    o_b, s_b = chunk_recurrence(flip(qf), flip(k_b), flip(vf), flip(logf_b), s0f[:, 1])
    o = rmsnorm(o_f + flip(o_b), g_norm.reshape(REC_HEADS, REC_DV)).reshape(B, T, REC_HEADS * REC_DV)
    o = o.astype(h.dtype) * jax.nn.silu(g)
    return o @ w_out, jnp.stack([s_f, s_b], axis=1)


def setup_inputs(seed: int = 0) -> dict:
    key = jax.random.key(seed)
    ks = jax.random.split(key, 20)
    nrm = lambda k, shape, s: jax.random.normal(k, shape, jnp.float32) * s
    D = D_MODEL
    return {
        'x_prompt': nrm(ks[0], (BATCH, SEQ, D), 1.0),
        'x_sample': nrm(ks[1], (DEC_BATCH, DEC_SEQ, D), 1.0),
        'c': nrm(ks[2], (DEC_BATCH, D), 1.0),
        'cache_k': nrm(ks[3], (DEC_BATCH, N_ATTN_LAYERS, PAST_LEN, N_KV_HEADS, HEAD_DIM), 1.0),
        'cache_v': nrm(ks[4], (DEC_BATCH, N_ATTN_LAYERS, PAST_LEN, N_KV_HEADS, HEAD_DIM), 1.0),
        'state_s': nrm(ks[5], (DEC_BATCH, N_REC_LAYERS, 2, REC_HEADS, REC_DK, REC_DV), 0.5),
        'c_ctx': nrm(ks[6], (D,), 1.0),
        'w_ada': nrm(ks[7], (DEPTH, D, 9 * D), 0.5 * D ** -0.5),
        'b_ada': nrm(ks[8], (DEPTH, 9 * D), 0.01),
        'norm_pre': 1.0 + nrm(ks[9], (DEPTH, 3, D), 0.02),
        'norm_post': 1.0 + nrm(ks[10], (DEPTH, 3, D), 0.02),
        'w_ffn_in': nrm(ks[11], (DEPTH, 2, D, 2 * D_FF), D ** -0.5),
        'w_ffn_out': nrm(ks[12], (DEPTH, 2, D_FF, D), D_FF ** -0.5),
        'w_qkv': nrm(ks[13], (N_ATTN_LAYERS, D, QKV_DIM), D ** -0.5),
        'w_attn_out': nrm(ks[14], (N_ATTN_LAYERS, N_Q_HEADS * HEAD_DIM, D), D ** -0.5),
        'attn_sink': nrm(ks[15], (N_ATTN_LAYERS, N_Q_HEADS), 0.5),
        'w_rec_in': nrm(ks[16], (N_REC_LAYERS, D, REC_IN_DIM), D ** -0.5),
        'rec_lb_logits': nrm(ks[17], (2, DEPTH, REC_HEADS * REC_DK), 0.1),
        'rec_norm': 1.0 + nrm(ks[18], (N_REC_LAYERS, REC_HEADS * REC_DV), 0.02),
        'w_rec_out': nrm(ks[19], (N_REC_LAYERS, REC_HEADS * REC_DV, D), D ** -0.5),
    }


def reference(x_prompt, x_sample, c, cache_k, cache_v, state_s, c_ctx, w_ada, b_ada, norm_pre, norm_post,
              w_ffn_in, w_ffn_out, w_qkv, w_attn_out, attn_sink, w_rec_in, rec_lb_logits, rec_norm, w_rec_out):
    xp, xs = x_prompt, x_sample
    Bp = xp.shape[0]
    Bs, T = xs.shape[:2]
    lb_soft = jax.nn.softmax(rec_lb_logits.astype(jnp.float32), axis=1)
    lb_all = jnp.cumsum(lb_soft, axis=1) - lb_soft[:, :1]
    new_k, new_v, new_s = [], [], []
    for i in range(DEPTH):
        mod_p = (jax.nn.silu(c_ctx) @ w_ada[i] + b_ada[i]).reshape(3, 3, D_MODEL)
        mod_s = (jax.nn.silu(c) @ w_ada[i] + b_ada[i]).reshape(Bs, 1, 3, 3, D_MODEL)
        xp = ffn_sublayer(xp, norm_pre[i, 0], norm_post[i, 0], mod_p, 0, w_ffn_in[i, 0], w_ffn_out[i, 0])
        xs = ffn_sublayer(xs, norm_pre[i, 0], norm_post[i, 0], mod_s, 0, w_ffn_in[i, 0], w_ffn_out[i, 0])
        hp = sublayer_in(xp, norm_pre[i, 1], mod_p, 1)
        hs = sublayer_in(xs, norm_pre[i, 1], mod_s, 1)
        j = i // N_MIXERS
        if i % N_MIXERS == 0:
            qp, kp, vp = attn_project(hp, w_qkv[j])
            op = ctx_attention(qp, kp, vp, attn_sink[j]) @ w_attn_out[j]
            qs, ks_, vs = attn_project(hs, w_qkv[j])
            qs = axial_rope(qs, T)
            ks_ = axial_rope(ks_, T)
            os_ = latent_attention(qs, ks_, vs, cache_k[:, j], cache_v[:, j], attn_sink[j]) @ w_attn_out[j]
            new_k.append(kp)
            new_v.append(vp)
        else:
            s_zero = jnp.zeros((Bp, 2, REC_HEADS, REC_DK, REC_DV), jnp.float32)
            op, sp = hgrn_mixer(hp, s_zero, w_rec_in[j], lb_all[0, i], lb_all[1, i], rec_norm[j], w_rec_out[j])
            os_, _ = hgrn_mixer(hs, state_s[:, j], w_rec_in[j], lb_all[0, i], lb_all[1, i], rec_norm[j], w_rec_out[j])
            new_s.append(sp)
        xp = sublayer_out(xp, op, norm_post[i, 1], mod_p, 1, 1.0)
        xs = sublayer_out(xs, os_, norm_post[i, 1], mod_s, 1, 1.0)
        xp = ffn_sublayer(xp, norm_pre[i, 2], norm_post[i, 2], mod_p, 2, w_ffn_in[i, 1], w_ffn_out[i, 1])
        xs = ffn_sublayer(xs, norm_pre[i, 2], norm_post[i, 2], mod_s, 2, w_ffn_in[i, 1], w_ffn_out[i, 1])
    new_cache_k = jnp.stack(new_k, axis=1)
    new_cache_v = jnp.stack(new_v, axis=1)
    new_state_s = jnp.stack(new_s, axis=1)
    return (xp, xs, new_cache_k, new_cache_v, new_state_s)
```

```python
import numpy as np
import ml_dtypes
import concourse.bass as bass
import concourse.mybir as mybir
from concourse.bass_utils import run_bass_kernel_spmd

F32 = mybir.dt.float32
BF16 = mybir.dt.bfloat16
AF = mybir.ActivationFunctionType
ALU = mybir.AluOpType

NCORES = 8
D = 1024
KC = 8
NTOK = 1280
NBLK = 10
DFF = 2816
NJ = 22
EPS = 1e-6
TILES = [(0, 512), (512, 1024), (1024, 1280)]


class _Op:
    __slots__ = ("eng", "emit", "edeps", "ddeps", "sig", "sigcount", "dma", "waits", "gidx")


class Prog:
    ENGS = ["pe", "act", "dve", "pool", "sp"]

    def __init__(self, nc):
        self.nc = nc
        self.ops = {e: [] for e in self.ENGS}
        self.order = []
        self.recs = {}
        self.dma_cum = {}
        self.elsize = {}

    @staticmethod
    def _region(ap):
        t = ap.tensor
        if type(t).__name__.startswith("DRam"):
            return None
        aps = ap.ap
        pstep, pn = aps[0]
        off = int(ap.offset)
        p0 = off // pstep
        f0 = off % pstep
        f1 = f0 + 1
        for s, n in aps[1:]:
            if s >= 0:
                f1 += (n - 1) * s
            else:
                f0 += (n - 1) * s
        if t.name.startswith("ps"):
            return (t.name, 0, 128, 0, 512)
        return (t.name, p0, p0 + pn, f0, f1)

    def add(self, eng, emit, reads=(), writes=(), dma=None):
        op = _Op()
        op.eng = eng
        op.emit = emit
        op.edeps = {}
        op.ddeps = {}
        op.sig = False
        op.sigcount = 0
        op.dma = None
        op.gidx = len(self.order)
        rregs = [r for r in (self._region(a) for a in reads) if r is not None]
        wregs = [r for r in (self._region(a) for a in writes) if r is not None]

        def dep(rec, raw):
            o = rec[5]
            if o.dma is not None:
                key = o.dma[0]
                op.ddeps[key] = max(op.ddeps.get(key, 0), self.dma_cum[key])
            else:
                op.edeps[o] = op.edeps.get(o, False) or raw

        for (name, p0, p1, f0, f1) in rregs:
            isps = name.startswith("ps")
            for rec in self.recs.get(name, ()):
                if (rec[0] or (isps and rec[5].eng != eng)) and rec[1] < p1 and p0 < rec[2] and rec[3] < f1 and f0 < rec[4]:
                    dep(rec, True)
        for (name, p0, p1, f0, f1) in wregs:
            for rec in self.recs.get(name, ()):
                if rec[1] < p1 and p0 < rec[2] and rec[3] < f1 and f0 < rec[4]:
                    dep(rec, False)
        for (name, p0, p1, f0, f1) in wregs:
            lst = self.recs.setdefault(name, [])
            lst[:] = [r for r in lst if not (p0 <= r[1] and r[2] <= p1 and f0 <= r[3] and r[4] <= f1)]
            lst.append((True, p0, p1, f0, f1, op))
        for (name, p0, p1, f0, f1) in rregs:
            lst = self.recs.setdefault(name, [])
            lst[:] = [r for r in lst if not ((not r[0]) and r[5].eng == eng and r[5].dma is None and dma is None
                                             and p0 <= r[1] and r[2] <= p1 and f0 <= r[3] and r[4] <= f1)]
            lst.append((False, p0, p1, f0, f1, op))
        if dma is not None:
            self.dma_cum[dma] = self.dma_cum.get(dma, 0) + 16
            op.dma = (dma, self.dma_cum[dma])
        self.ops[eng].append(op)
        self.order.append(op)
        return op

    def resolve(self):
        for op in self.order:
            keep = []
            for d, raw in op.edeps.items():
                if d.eng == op.eng and op.eng == "pe" and op.dma is None:
                    continue
                d.sig = True
                keep.append(d)
            op.edeps = keep
        for e in self.ENGS:
            c = 0
            for o in self.ops[e]:
                if o.sig:
                    c += 1
                o.sigcount = c
        self.eng_total = {e: (self.ops[e][-1].sigcount if self.ops[e] else 0) for e in self.ENGS}
        for e in self.ENGS:
            waited = {}
            for o in self.ops[e]:
                w = {}
                for d in o.edeps:
                    k = ("e", d.eng)
                    w[k] = max(w.get(k, 0), d.sigcount)
                for key, v in o.ddeps.items():
                    k = ("d", key)
                    w[k] = max(w.get(k, 0), v)
                o.waits = []
                for k, v in w.items():
                    if waited.get(k, 0) < v:
                        waited[k] = v
                        o.waits.append((k, v))

    def emit_all(self, final_wait_eng="sp"):
        nc = self.nc
        self.resolve()
        import contextlib
        with contextlib.ExitStack() as st:
            esem = {e: st.enter_context(nc.semaphore("sem_" + e)) for e in self.ENGS}
            dsem = {k: st.enter_context(nc.semaphore("dsem_" + k)) for k in self.dma_cum}
            block = st.enter_context(nc.Block())

            def run(e, eng):
                for o in self.ops[e]:
                    for (k, v) in o.waits:
                        sem = esem[k[1]] if k[0] == "e" else dsem[k[1]]
                        eng.wait_ge(sem, v)
                    ins = o.emit(eng)
                    if o.dma is not None:
                        ins.then_inc(dsem[o.dma[0]], 16)
                    elif o.sig:
                        ins.then_inc(esem[e], 1)
                if e == final_wait_eng:
                    for k, v in self.dma_cum.items():
                        eng.wait_ge(dsem[k], v)

            @block.tensor
            def _(eng):
                run("pe", eng)

            @block.scalar
            def _(eng):
                run("act", eng)

            @block.vector
            def _(eng):
                run("dve", eng)

            @block.gpsimd
            def _(eng):
                run("pool", eng)

            @block.sync
            def _(eng):
                run("sp", eng)


class Builder:
    def __init__(self, nsub=6):
        self.nsub = nsub
        self.nc = bass.Bass("TRN2", target_bir_lowering=False)
        self.P = Prog(self.nc)
        self.stg_i = 0
        self.stgb_i = 0
        self.ada_next = {}
        self.hook = None
        self.next_pre = None
        self.pre_done = False
        self.pre_r = {}
        self.ps_rr = 0
        self.tmp_rr = {}

    def mm(self, out, lhsT, rhs, start, stop):
        self.P.add("pe", lambda e: e.matmul(out, lhsT, rhs, start=start, stop=stop), reads=[lhsT, rhs], writes=[out])

    def act(self, out, in_, func, bias=None, scale=1.0):
        reads = [in_]
        kw = {}
        if bias is not None:
            kw["bias"] = bias
            if not isinstance(bias, float):
                reads.append(bias)
        if not isinstance(scale, float):
            reads.append(scale)
        self.P.add("act", lambda e: e.activation(out=out, in_=in_, func=func, scale=scale, **kw), reads=reads, writes=[out])

    def tt(self, eng, out, in0, in1, op):
        self.P.add(eng, lambda e: e.tensor_tensor(out=out, in0=in0, in1=in1, op=op), reads=[in0, in1], writes=[out])

    def stt(self, eng, out, in0, scalar, in1, op0, op1):
        reads = [in0, in1] + ([] if isinstance(scalar, float) else [scalar])
        self.P.add(eng, lambda e: e.scalar_tensor_tensor(out=out, in0=in0, scalar=scalar, in1=in1, op0=op0, op1=op1),
                   reads=reads, writes=[out])

    def ts(self, eng, out, in0, s1, s2, op0, op1=None):
        reads = [in0] + [s for s in (s1, s2) if s is not None and not isinstance(s, float)]
        if op1 is None:
            self.P.add(eng, lambda e: e.tensor_scalar(out=out, in0=in0, scalar1=s1, scalar2=None, op0=op0),
                       reads=reads, writes=[out])
        else:
            self.P.add(eng, lambda e: e.tensor_scalar(out=out, in0=in0, scalar1=s1, scalar2=s2, op0=op0, op1=op1),
                       reads=reads, writes=[out])

    def copy(self, eng, out, in_):
        self.P.add(eng, lambda e: e.tensor_copy(out=out, in_=in_), reads=[in_], writes=[out])

    def recip(self, out, in_):
        self.P.add("dve", lambda e: e.reciprocal(out=out, in_=in_), reads=[in_], writes=[out])

    def memset(self, eng, ap, val):
        self.P.add(eng, lambda e: e.memset(ap, val), reads=[], writes=[ap])

    def dma(self, out, in_, slot, eng="sp"):
        self.P.add(eng, lambda e: e.dma_start(out=out, in_=in_), reads=[in_], writes=[out], dma=slot)

    def alloc(self):
        nc = self.nc
        import contextlib
        self.st = contextlib.ExitStack()
        sb = lambda name, shape, dt: self.st.enter_context(nc.sbuf_tensor(name, shape, dt))
        self.xT = sb("xT", [128, KC, NTOK], F32)
        self.hT = sb("hT", [128, KC, NTOK], BF16)
        self.mid = sb("mid", [128, NJ * NTOK], BF16)
        self.outb = sb("outb", [128, KC * NTOK], F32)
        self.NB = 6
        self.stgb = [sb("stgb%d" % i, [128, 2048], BF16) for i in range(self.NB)]
        self.ps = [self.st.enter_context(nc.psum_tensor("ps%d" % i, [128, 512], F32)) for i in range(8)]
        self.onesD = sb("onesD", [128, 128], BF16)
        self.epsT = sb("epsT", [128, 1], F32)
        self.condT = sb("condT", [128, 2, KC], F32)
        self.scondb = sb("scondb", [128, KC, 2], BF16)
        self.badaT = sb("badaT", [128, 2, 72], F32)
        self.gpre = sb("gpre", [128, 48], F32)
        self.gpost = sb("gpost", [128, 48], F32)
        self.modT = sb("modT", [128, 2, 2, 72], F32)
        self.Av = sb("Av", [128, 2, 2, 24], F32)
        self.Gv = sb("Gv", [128, 2, 2, 24], F32)
        self.rs = [sb("rs%d" % i, [128, 512], F32) for i in range(2)]
        self.rstd = [sb("rstd%d" % i, [128, 512], F32) for i in range(2)]
        self.tmp = [sb("tmp%d" % i, [128, 512], F32) for i in range(3)]
        self.sq = [sb("sq%d" % i, [128, 512], BF16) for i in range(2)]
        self.ctxb = sb("ctxb", [128, NBLK], F32)
        self.lbl = sb("lbl", [128, 2, 2, 8], F32)
        self.lb = sb("lb", [128, 2, 8], F32)
        self.lbs = sb("lbs", [128, 2, 8], F32)
        self.nlbs = sb("nlbs", [128, 2, 8], F32)
        self.gnorm = sb("gnorm", [128, 8], F32)
        self.keep = sb("keep", [128, 1], F32)
        self.amask = sb("amask", [128, 2, 64], F32)
        self.decv = sb("decv", [128, 2, 20], F32)
        self.Sf = sb("Sf", [128, 2, 2, 128], F32)
        self.u2 = [sb("u2_%d" % i, [128, 128], F32) for i in range(3)]
        self.pu_i = 0
        self.onesV = sb("onesV", [128, 128], BF16)
        self.identb = sb("identb", [128, 128], BF16)

    def rr(self, name, lst):
        i = self.tmp_rr.get(name, 0)
        self.tmp_rr[name] = i + 1
        return lst[i % len(lst)]

    def dram_in(self, name, shape, dt=F32):
        return self.nc.dram_tensor(name, list(shape), dt, kind="ExternalInput").ap()

    def dram_out(self, name, shape, dt=F32):
        return self.nc.dram_tensor(name, list(shape), dt, kind="ExternalOutput").ap()

    def load_w(self, src_ap, n):
        s = self.stg[self.stg_i % 2]
        self.stg_i += 1
        self.dma(s[:, 0:n], src_ap, slot="stg%d" % ((self.stg_i - 1) % 2))
        return s

    def load_w_bf(self, src_ap, n):
        k = self.stgb_i % self.NB
        self.stgb_i += 1
        b = self.stgb[k]
        self.dma(b[:, 0:n], src_ap, slot="stgb%d" % k, eng="pool")
        return b

    def adaln_units(self, li, n):
        mp = self.ps[7]
        st = self.ada_next.get(li, 0)
        for u in range(st, min(36, st + n)):
            wb = self.load_w_bf(self.w_ada[li, u], 2048)
            for b in range(2):
                blk = u * 2 + b
                for kc in range(KC):
                    self.mm(mp[:, blk * 2:blk * 2 + 2], wb[:, (b * 8 + kc) * 128:(b * 8 + kc + 1) * 128],
                            self.scondb[:, kc, :], start=(kc == 0), stop=(kc == KC - 1))
        self.ada_next[li] = min(36, st + n)

    def adaln_fin(self, li, b0, b1):
        mp3 = self.ps[7][:, 0:144].rearrange("p (b t) -> p b t", t=2)
        for v in range(2):
            self.tt("dve", self.modT[:, li, v, b0:b1], mp3[:, b0:b1, v], self.badaT[:, li, b0:b1], ALU.add)
        for v in range(2):
            for s in range(3):
                wgt = 1.0 if s == 1 else 0.5
                gp = self.gpre[:, (li * 3 + s) * 8:(li * 3 + s + 1) * 8]
                gq = self.gpost[:, (li * 3 + s) * 8:(li * 3 + s + 1) * 8]
                if b0 <= (s * 3 + 1) * 8 and (s * 3 + 2) * 8 <= b1:
                    sc = self.modT[:, li, v, (s * 3 + 1) * 8:(s * 3 + 2) * 8]
                    self.stt("dve", self.Av[:, li, v, s * 8:(s + 1) * 8], sc, 1.0, gp, ALU.add, ALU.mult)
                if b0 <= (s * 3 + 2) * 8 and (s * 3 + 3) * 8 <= b1:
                    gt = self.modT[:, li, v, (s * 3 + 2) * 8:(s * 3 + 3) * 8]
                    self.stt("dve", self.Gv[:, li, v, s * 8:(s + 1) * 8], gt, wgt, gq, ALU.mult, ALU.mult)

    def rstd_from(self, ss_ps, n, rbuf=None):
        rstd = rbuf if rbuf is not None else self.rr("rstd", self.rstd)
        self.act(rstd[:, :n], ss_ps[:, :n], AF.Ln, bias=self.epsT[:, 0:1])
        self.act(rstd[:, :n], rstd[:, :n], AF.Exp, scale=-0.5)
        return rstd

    def prenorm(self, li, s, ti, phase=None, rbuf=None):
        c0, c1 = TILES[ti]
        n = c1 - c0
        if phase in (None, "stats"):
            ssp = self.ps[2] if ti % 2 == 0 else self.ps[3]
            for kc in range(KC):
                sq = self.rr("sq", self.sq)
                self.act(sq[:, :n], self.xT[:, kc, c0:c1], AF.Square)
                self.mm(ssp[:, :n], self.onesD[:, :], sq[:, :n], start=(kc == 0), stop=(kc == KC - 1))
            self.pre_r[ti] = self.rstd_from(ssp, n, rbuf)
        if phase == "stats":
            return
        rstd = self.pre_r[ti]
        for kc in range(KC):
            t = self.rr("tmp", self.tmp)
            v = 1 if ti == 2 else 0
            if kc in (2, 6):
                self.tt("pool", t[:, :n], self.xT[:, kc, c0:c1], rstd[:, :n], ALU.mult)
                self.act(self.hT[:, kc, c0:c1], t[:, :n], AF.Identity,
                         bias=self.modT[:, li, v, (s * 3) * 8 + kc:(s * 3) * 8 + kc + 1],
                         scale=self.Av[:, li, v, s * 8 + kc:s * 8 + kc + 1])
            else:
                self.stt("dve", t[:, :n], self.xT[:, kc, c0:c1], self.Av[:, li, v, s * 8 + kc:s * 8 + kc + 1],
                         rstd[:, :n], ALU.mult, ALU.mult)
                self.act(self.hT[:, kc, c0:c1], t[:, :n], AF.Identity,
                         bias=self.modT[:, li, v, (s * 3) * 8 + kc:(s * 3) * 8 + kc + 1])

    def maybe_prenorm(self, li, s):
        if self.pre_done:
            self.pre_done = False
            return
        for ti in range(3):
            self.prenorm(li, s, ti)

    def postnorm(self, li, s, ti, ssp):
        c0, c1 = TILES[ti]
        n = c1 - c0
        outT = self.outb[:, :].rearrange("p (k t) -> p k t", t=NTOK)
        rstd = self.rstd_from(ssp, n)
        for m in range(KC):
            t = self.rr("tmp", self.tmp)
            v = 1 if ti == 2 else 0
            self.stt("dve", t[:, :n], outT[:, m, c0:c1], self.Gv[:, li, v, s * 8 + m:s * 8 + m + 1], rstd[:, :n],
                     ALU.mult, ALU.mult)
            self.tt("dve" if m % 2 == 1 else "pool", self.xT[:, m, c0:c1], self.xT[:, m, c0:c1], t[:, :n], ALU.add)

    def out_proj(self, li, s, nk, units, rhs_fn):
        outT = self.outb[:, :].rearrange("p (k t) -> p k t", t=NTOK)
        ssb = [self.ps[4], self.ps[5], self.ps[6]]
        it = 0
        for m in range(KC):
            wb = []
            for (ap, nkc) in units[m]:
                b = self.load_w_bf(ap, nkc * 128)
                for k in range(nkc):
                    wb.append(b[:, k * 128:(k + 1) * 128])
            assert len(wb) == nk
            for ti, (c0, c1) in enumerate(TILES):
                n = c1 - c0
                po = self.ps[it % 2]
                it += 1
                for k in range(nk):
                    self.mm(po[:, :n], wb[k], rhs_fn(k, c0, c1), start=(k == 0), stop=(k == nk - 1))
                sq = self.rr("sq", self.sq)
                self.copy("dve", outT[:, m, c0:c1], po[:, :n])
                self.act(sq[:, :n], outT[:, m, c0:c1], AF.Square)
                self.mm(ssb[ti][:, :n], self.onesD[:, :], sq[:, :n], start=(m == 0), stop=(m == KC - 1))
            if self.hook is not None:
                self.hook("out")
        if self.hook is not None:
            self.hook("fin")
        if getattr(self, "dbg", 9) < 5:
            return
        if self.next_pre is not None:
            self.postnorm(li, s, 0, ssb[0])
            self.postnorm(li, s, 1, ssb[1])
            self.next_pre(0)
            self.postnorm(li, s, 2, ssb[2])
            self.next_pre(1)
            self.next_pre(2)
            self.pre_done = True
        else:
            last = getattr(self, "is_last", False)
            for ti in range(3):
                self.postnorm(li, s, ti, ssb[ti])
                if last:
                    c0, c1 = TILES[ti]
                    for kc in range(KC):
                        self.dma(self.yT[:, kc, c0:c1], self.xT[:, kc, c0:c1], slot="yout")
            if last:
                self.y_stored = True

    def ffn(self, li, f, s):
        mid = self.mid[:, :].rearrange("p (j t) -> p j t", t=NTOK)
        cnt = {"it": 0}

        def grp(j, wb, ti):
            c0, c1 = TILES[ti]
            n = c1 - c0
            pa = self.ps[(cnt["it"] % 2) * 2]
            pb = self.ps[(cnt["it"] % 2) * 2 + 1]
            cnt["it"] += 1
            for kc in range(KC):
                self.mm(pa[:, :n], wb[:, kc * 256:kc * 256 + 128], self.hT[:, kc, c0:c1], start=(kc == 0),
                        stop=(kc == KC - 1))
            for kc in range(KC):
                self.mm(pb[:, :n], wb[:, kc * 256 + 128:kc * 256 + 256], self.hT[:, kc, c0:c1], start=(kc == 0),
                        stop=(kc == KC - 1))
            t = self.rr("tmp", self.tmp)
            self.act(t[:, :n], pa[:, :n], AF.Silu)
            self.tt("dve", mid[:, j, c0:c1], t[:, :n], pb[:, :n], ALU.mult)

        JB = 3
        wbs = [self.load_w_bf(self.w_in[li, f, j], 2048) for j in range(JB)]
        for ti in range(3):
            for j in range(JB):
                grp(j, wbs[j], ti)
        for j in range(JB):
            if self.hook is not None:
                self.hook("in")
        for j in range(JB, NJ):
            wb = self.load_w_bf(self.w_in[li, f, j], 2048)
            for ti in range(3):
                grp(j, wb, ti)
            if self.hook is not None:
                self.hook("in")
        if getattr(self, "dbg", 9) < 4:
            return
        units = [[(self.w_out[li, f, m * 2 + h], 11) for h in range(2)] for m in range(KC)]
        self.out_proj(li, s, NJ, units, lambda k, c0, c1: mid[:, k, c0:c1])


    def carve_reset(self):
        self.mid_off = 0
        self.outb_off = 0

    def carve_mid(self, n):
        a = self.mid[:, self.mid_off:self.mid_off + n]
        self.mid_off += n
        assert self.mid_off <= NJ * NTOK
        return a

    def carve_outb(self, n):
        a = self.outb[:, self.outb_off:self.outb_off + n]
        self.outb_off += n
        assert self.outb_off <= KC * NTOK
        return a

    def bcast(self, ap2d, reps):
        (ps, pn), (s1, n1) = ap2d.ap
        return bass.AP(ap2d.tensor, ap2d.offset, [[ps, pn], [0, reps], [s1, n1]])

    def attention(self, li, s):
        self.carve_reset()
        qT = self.carve_mid(8 * NTOK).rearrange("p (c t) -> p c t", t=NTOK)
        kT = self.carve_mid(2 * NTOK).rearrange("p (c t) -> p c t", t=NTOK)
        vaug = self.carve_mid(NBLK * 4 * 128).rearrange("p (b g d) -> p b g d", g=4, d=128)
        ckT = self.carve_mid(2 * 256).rearrange("p (c t) -> p c t", t=256)
        cvaug = self.carve_mid(2 * 4 * 128).rearrange("p (b g d) -> p b g d", g=4, d=128)
        PT = [self.carve_mid(512) for _ in range(6)]
        maskT = self.carve_mid(20 * 128).rearrange("p (m q) -> p m q", q=128)
        sinkrow = self.carve_mid(2048)
        ident = self.carve_mid(128)
        sinkL = self.carve_mid(256)
        ropeC = self.carve_outb(NTOK)
        ropeS = self.carve_outb(NTOK)
        ckf = self.carve_outb(512).rearrange("p (c t) -> p c t", t=256)
        cvf = self.carve_outb(512).rearrange("p (b f) -> p b f", f=256)
        kf = self.carve_outb(2 * NTOK).rearrange("p (c t) -> p c t", t=NTOK)
        vf = self.carve_outb(2 * 256).rearrange("p (b f) -> p b f", f=256)
        sinkf = self.carve_outb(2048)
        oT = self.hT
        self.dma(ropeC, self.ropeC_in, slot="small")
        self.dma(ropeS, self.ropeS_in, slot="small")
        self.dma(maskT, self.mask_in, slot="small")
        self.dma(ident, self.ident_in, slot="small")
        self.dma(sinkL[0:1, :], self.sinkL_in, slot="small")
        self.dma(sinkf[0:1, :], self.sink_in, slot="small")
        self.dma(ckf, self.ckT_in, slot="small")
        self.dma(cvf, self.cv_in, slot="small")
        self.act(sinkrow[0:1, :], sinkf[0:1, :], AF.Exp)
        self.copy("dve", ckT, ckf)
        self.memset("pool", vaug, 1.0)
        self.memset("pool", cvaug, 1.0)
        cv4 = cvf.rearrange("p b (g d) -> p b g d", d=64)
        for b in range(2):
            self.copy("dve", cvaug[:, b, 0::2, 0:64], cv4[:, b, 0::2, :])
            self.copy("dve", cvaug[:, b, 1::2, 64:128], cv4[:, b, 1::2, :])
        self.maybe_prenorm(li, s)
        acnt = {"it": 0}

        def qk_grp(i, wb, ti):
            c0, c1 = TILES[ti]
            n = c1 - c0
            pa = self.ps[(acnt["it"] % 2) * 2]
            pb = self.ps[(acnt["it"] % 2) * 2 + 1]
            acnt["it"] += 1
            for kc in range(KC):
                self.mm(pa[:, :n], wb[:, kc * 128:(kc + 1) * 128], self.hT[:, kc, c0:c1], start=(kc == 0),
                        stop=(kc == KC - 1))
            for kc in range(KC):
                self.mm(pb[:, :n], wb[:, 1024 + kc * 128:1024 + (kc + 1) * 128], self.hT[:, kc, c0:c1],
                        start=(kc == 0), stop=(kc == KC - 1))
            t1 = self.rr("tmp", self.tmp)
            t2 = self.rr("tmp", self.tmp)
            self.tt("dve", t1[:, :n], pa[:, :n], ropeC[:, c0:c1], ALU.mult)
            self.tt("dve", t2[:, :n], pb[:, :n], ropeS[:, c0:c1], ALU.mult)
            dst = qT[:, i, c0:c1] if i < 8 else kT[:, i - 8, c0:c1]
            if i >= 8:
                self.copy("dve", kf[:, i - 8, c0:c1], pa[:, :n])
            self.tt("pool", dst, t1[:, :n], t2[:, :n], ALU.add)

        wq = [self.load_w_bf(self.w_qk[i], 2048) for i in range(3)]
        for ti in range(3):
            for i in range(3):
                qk_grp(i, wq[i], ti)
        for i in range(3, 10):
            wb = self.load_w_bf(self.w_qk[i], 2048)
            for ti in range(3):
                qk_grp(i, wb, ti)
        for cch in range(2):
            self.dma(self.kout[:, cch, :], kf[:, cch, :], slot="kvout")
        wv = self.load_w_bf(self.w_v, 2048)
        for blk in range(NBLK):
            pv = self.ps[4 + blk % 2]
            for kc in range(KC):
                self.mm(pv[:, 0:256], self.hT[:, kc, blk * 128:(blk + 1) * 128], wv[:, kc * 256:(kc + 1) * 256],
                        start=(kc == 0), stop=(kc == KC - 1))
            self.act(vf[:, blk % 2, :], pv[:, 0:256], AF.Copy)
            v4 = vf[:, blk % 2, :].rearrange("p (g d) -> p g d", d=64)
            self.copy("dve", vaug[:, blk, 0::2, 0:64], v4[:, 0::2, :])
            self.copy("dve", vaug[:, blk, 1::2, 64:128], v4[:, 1::2, :])
            self.dma(self.vout[blk * 128:(blk + 1) * 128, :], vf[:, blk % 2, :], slot="kvout")
        sb_i = 0
        ob_i = 0
        for j in range(NBLK):
            for p in range(2):
                for e in range(2):
                    g = 2 * p + e
                    r0, r1 = e * 64, (e + 1) * 64
                    kbs = []
                    if j > 0:
                        kbs.append((kT[r0:r1, p, (j - 1) * 128:j * 128], vaug[:, j - 1, g, :], j * 2, None))
                    kbs.append((kT[r0:r1, p, j * 128:(j + 1) * 128], vaug[:, j, g, :], None, None))
                    if j < NBLK - 1:
                        kbs.append((kT[r0:r1, p, (j + 1) * 128:(j + 2) * 128], vaug[:, j + 1, g, :], j * 2 + 1, None))
                    for cb in range(2):
                        kbs.append((ckT[r0:r1, p, cb * 128:(cb + 1) * 128], cvaug[:, cb, g, :], None,
                                    self.ctxb[:, j:j + 1]))
                    qrhs = qT[r0:r1, p * 4:p * 4 + 4, j * 128:(j + 1) * 128]
                    pts = []
                    for (kap, vap, mi, bias) in kbs:
                        sp = self.ps[sb_i % 6]
                        pt = PT[sb_i % 6]
                        sb_i += 1
                        self.mm(sp[:, :], kap, qrhs, start=True, stop=(mi is None))
                        if mi is not None:
                            self.mm(sp[:, :], ident, self.bcast(maskT[:, mi, :], 4), start=False, stop=True)
                        self.act(pt, sp[:, :], AF.Exp, bias=bias, scale=0.125)
                        pts.append((vap, pt))
                    po = self.ps[6 + ob_i % 2]
                    ob_i += 1
                    for idx, (vap, pt) in enumerate(pts):
                        self.mm(po[:, :], vap, pt, start=(idx == 0), stop=False)
                    self.mm(po[:, :], sinkL[0:1, e * 128:(e + 1) * 128], sinkrow[0:1, g * 512:(g + 1) * 512],
                            start=False, stop=True)
                    rd = self.rr("tmp", self.tmp)
                    d0, d1 = (64, 128) if e == 0 else (0, 64)
                    self.recip(rd[r0:r1, :], po[d0:d1, :])
                    self.tt("dve", oT[r0:r1, p * 4:p * 4 + 4, j * 128:(j + 1) * 128], po[r0:r1, :], rd[r0:r1, :], ALU.mult)
        units = [[(self.w_o[m], 8)] for m in range(KC)]
        self.out_proj(li, s, 8, units, lambda k, c0, c1: oT[:, k, c0:c1])


    def scan(self, out, d0, d1):
        self.P.add("dve", lambda e: e.tensor_tensor_scan(out=out, data0=d0, data1=d1, initial=0.0, op0=ALU.mult, op1=ALU.add),
                   reads=[d0, d1], writes=[out])

    def rev(self, ap2d):
        (ps, pn), (s1, n1) = ap2d.ap
        return bass.AP(ap2d.tensor, ap2d.offset + (n1 - 1) * s1, [[ps, pn], [-s1, n1]])

    def hgrn(self, li, s):
        self.carve_reset()
        vtm = self.carve_mid(NBLK * 128).rearrange("p (b f) -> p b f", f=128)
        oT = self.carve_mid(8 * NTOK).rearrange("p (c t) -> p c t", t=NTOK)
        qd = self.carve_mid(2 * NTOK).rearrange("p (d t) -> p d t", t=NTOK)
        ki = self.carve_mid(2 * NTOK).rearrange("p (d t) -> p d t", t=NTOK)
        kitm = self.carve_mid(2 * NBLK * 128).rearrange("p (d b k) -> p d b k", b=NBLK, k=128)
        AT = self.carve_mid(2 * 640).rearrange("p (d c) -> p d c", c=640)
        Sst = self.carve_mid(5 * 4 * 2 * 128).rearrange("p (g c d v) -> p g c d v", c=4, d=2, v=128)
        ident = self.identb[:, :]
        mvec = self.carve_mid(2 * NTOK).rearrange("p (d t) -> p d t", t=NTOK)
        qs, sg, sf, sb_, lg, kk, bc, eb = [self.carve_outb(NTOK) for _ in range(8)]
        self.dma(ident, self.ident_in, slot="small")
        self.dma(mvec, self.mvec_in, slot="small")
        self.dma(self.lbl[:, :, :, :], self.lbl_in, slot="small")
        self.dma(self.gnorm[:, :], self.gnorm_in, slot="small")
        self.dma(self.keep[:, :], self.keep_in, slot="small")
        self.dma(self.amask[:, :, :], self.amask_in, slot="small")
        self.memset("dve", self.onesV[:, :], 1.0 / 128)
        self.tt("dve", self.lb[:, :, :], self.lbl[:, :, 1, :], self.lbl[:, :, 0, :], ALU.subtract)
        self.act(self.lb[:, :, :], self.lb[:, :, :], AF.Sigmoid)
        self.ts("dve", self.lbs[:, :, :], self.lb[:, :, :], -1.0, 1.0, ALU.mult, ALU.add)
        self.ts("dve", self.nlbs[:, :, :], self.lbs[:, :, :], -1.0, None, ALU.mult)
        self.maybe_prenorm(li, s)
        def proj_one(hd, wsrc, half, dst, fn, bank0):
            for ti, (c0, c1) in enumerate(TILES):
                n = c1 - c0
                pp = self.ps[bank0 + ti % 2]
                for kc in range(KC):
                    self.mm(pp[:, :n], wsrc[:, half * 1024 + kc * 128:half * 1024 + (kc + 1) * 128],
                            self.hT[:, kc, c0:c1], start=(kc == 0), stop=(kc == KC - 1))
                self.act(dst[:, c0:c1], pp[:, :n], fn)

        wcache = {}

        def proj_grp(hd, wsrc, half, dst, fn, ti):
            c0, c1 = TILES[ti]
            n = c1 - c0
            pp = self.ps[ti % 2]
            for kc in range(KC):
                self.mm(pp[:, :n], wsrc[:, half * 1024 + kc * 128:half * 1024 + (kc + 1) * 128],
                        self.hT[:, kc, c0:c1], start=(kc == 0), stop=(kc == KC - 1))
            self.act(dst[:, c0:c1], pp[:, :n], fn)

        def proj_qz_groups(hd):
            w1 = self.load_w_bf(self.w_rf[hd, 0], 2048)
            w2 = self.load_w_bf(self.w_rf[hd, 1], 2048)
            wcache[hd] = w1
            gl = []
            for (wsrc, half, dst, fn) in ((w1, 0, qs, AF.Silu), (w2, 0, sf, AF.Sigmoid), (w2, 1, sb_, AF.Sigmoid)):
                for ti in range(3):
                    gl.append((hd, wsrc, half, dst, fn, ti))
            return gl

        def proj_qz(hd):
            for g_ in proj_qz_groups(hd):
                proj_grp(*g_)

        def proj_gv(hd):
            w1 = wcache[hd]
            proj_one(hd, w1, 1, sg, AF.Silu, 0)
            wv = self.load_w_bf(self.w_rfv[hd], 1024)
            for b0 in range(0, NBLK, 4):
                nb = min(4, NBLK - b0)
                pv = self.ps[2]
                for bb in range(nb):
                    blk = b0 + bb
                    for kc in range(KC):
                        self.mm(pv[:, bb * 128:(bb + 1) * 128], self.hT[:, kc, blk * 128:(blk + 1) * 128],
                                wv[:, kc * 128:(kc + 1) * 128], start=(kc == 0), stop=(kc == KC - 1))
                self.copy("dve", vtm[:, b0:b0 + nb, :], pv[:, 0:nb * 128].rearrange("p (b k) -> p b k", k=128))

        def gates(hd):
            for d in range(2):
                sgm = sf if d == 0 else sb_
                self.act(lg, sgm, AF.Ln, bias=self.lb[:, d, hd:hd + 1], scale=self.lbs[:, d, hd:hd + 1])
                self.ts("dve", kk, sgm, self.nlbs[:, d, hd:hd + 1], self.lbs[:, d, hd:hd + 1], ALU.mult, ALU.add)
                if d == 0:
                    self.scan(bc, mvec[:, 0, :], lg)
                else:
                    self.scan(self.rev(bc), self.rev(mvec[:, 1, :]), self.rev(lg))
                self.act(lg, bc, AF.Exp, scale=-1.0)
                self.act(eb, bc, AF.Exp)
                self.tt("dve", ki[:, d, :], kk, lg, ALU.mult)
                if d == 0:
                    self.copy("dve", self.decv[:, 0, :], eb[:, 63::64])
                else:
                    self.copy("dve", self.decv[:, 1, :], eb[:, 0::64])
                self.stt("dve", qd[:, d, :], qs, 128 ** -0.5, eb, ALU.mult, ALU.mult)
                for b0 in range(0, NBLK, 4):
                    nb = min(4, NBLK - b0)
                    pt = self.ps[4 + (b0 // 4) % 2]
                    for bb in range(nb):
                        blk = b0 + bb
                        self.mm(pt[:, bb * 128:(bb + 1) * 128], ki[:, d, blk * 128:(blk + 1) * 128], ident,
                                start=True, stop=True)
                    self.act(kitm[:, d, b0:b0 + nb, :], pt[:, 0:nb * 128].rearrange("p (b k) -> p b k", k=128), AF.Copy)
                pA, pB = self.ps[6], self.ps[7]
                for blk in range(NBLK):
                    for par in range(2):
                        ch = blk * 2 + par
                        dstp = pA[par * 64:(par + 1) * 64, blk * 64:(blk + 1) * 64] if blk < 8 else \
                            pB[par * 64:(par + 1) * 64, (blk - 8) * 64:(blk - 7) * 64]
                        self.mm(dstp, ki[:, d, ch * 64:(ch + 1) * 64], qd[:, d, ch * 64:(ch + 1) * 64], start=True, stop=True)
                self.tt("dve", AT[:, d, 0:512].rearrange("p (b c) -> p b c", c=64),
                        pA[:, 0:512].rearrange("p (b c) -> p b c", c=64), self.bcast(self.amask[:, d, :], 8), ALU.mult)
                self.tt("dve", AT[:, d, 512:640].rearrange("p (b c) -> p b c", c=64),
                        pB[:, 0:128].rearrange("p (b c) -> p b c", c=64), self.bcast(self.amask[:, d, :], 2), ALU.mult)

        def state(hd, pending=()):
            pending = list(pending)
            orders = []
            for d in range(2):
                seg_order = [0, 1, 2, 3, 4] if d == 0 else [3, 2, 1, 0, 4]
                ch_order = [0, 1, 2, 3] if d == 0 else [3, 2, 1, 0]
                orders.append([(si, seg, ci, ch) for si, seg in enumerate(seg_order) for ci, ch in enumerate(ch_order)])
            cur = [0, 0]
            for step in range(20):
                for d in range(2):
                    si, seg, ci, ch = orders[d][step]
                    S_cur = self.Sf[:, d, cur[d], :]
                    if ci == 0:
                        if si == 0:
                            self.dma(S_cur, self.sinit_in[hd, d], slot="sinit%d" % d)
                        elif seg == 4:
                            self.memset("dve", S_cur, 0.0)
                        else:
                            S_prev = S_cur
                            cur[d] ^= 1
                            S_cur = self.Sf[:, d, cur[d], :]
                            self.ts("dve", S_cur, S_prev, self.keep[:, 0:1], None, ALU.mult)
                    chunk = seg * 4 + ch
                    blk, par = chunk // 2, chunk % 2
                    r0, r1 = par * 64, par * 64 + 64
                    pu = self.ps[3 + self.pu_i % 3]
                    self.pu_i += 1
                    self.mm(pu[:, 0:128], kitm[r0:r1, d, blk, :], vtm[r0:r1, blk, :], start=True, stop=True)
                    u2 = self.rr("u2", self.u2)
                    self.act(u2[:, :], pu[:, 0:128], AF.Copy, scale=self.decv[:, d, chunk:chunk + 1])
                    self.copy("pool" if step % 2 == 0 else "dve", Sst[:, seg, ch, d, :], S_cur)
                    S_nxt = self.Sf[:, d, cur[d] ^ 1, :]
                    self.stt("dve", S_nxt, S_cur, self.decv[:, d, chunk:chunk + 1], u2[:, :], ALU.mult, ALU.add)
                    cur[d] ^= 1
                    if ci == 3:
                        t = self.rr("tmp", self.tmp)
                        self.copy("pool", t[:, 0:128], S_nxt)
                        self.dma(self.sout[seg, d, hd], t[:, 0:128], slot="sout")
                if step % 2 == 1 and pending:
                    proj_grp(*pending.pop(0))
            while pending:
                proj_grp(*pending.pop(0))

        def output(hd):
            for ti, (c0, c1) in enumerate(TILES):
                n = c1 - c0
                po = self.ps[6 + ti % 2]
                for chunk in range(c0 // 64, c1 // 64):
                    seg, ch = chunk // 4, chunk % 4
                    blk, par = chunk // 2, chunk % 2
                    r0, r1 = par * 64, par * 64 + 64
                    col = chunk * 64 - c0
                    dst = po[:, col:col + 64]
                    vl = vtm[r0:r1, blk, :]
                    self.mm(dst, vl, AT[r0:r1, 0, blk * 64:(blk + 1) * 64], start=True, stop=False)
                    self.mm(dst, Sst[:, seg, ch, 0, :], qd[:, 0, chunk * 64:(chunk + 1) * 64], start=False, stop=False)
                    self.mm(dst, vl, AT[r0:r1, 1, blk * 64:(blk + 1) * 64], start=False, stop=False)
                    self.mm(dst, Sst[:, seg, ch, 1, :], qd[:, 1, chunk * 64:(chunk + 1) * 64], start=False, stop=True)
                osb = bc[:, c0:c1]
                self.act(osb, po[:, :n], AF.Copy)
                sq = self.rr("sq", self.sq)
                self.act(sq[:, :n], osb, AF.Square)
                ssp = self.ps[4 + ti % 2]
                self.mm(ssp[:, :n], self.onesV[:, :], sq[:, :n], start=True, stop=True)
                rstd = self.rstd_from(ssp, n)
                t = self.rr("tmp", self.tmp)
                self.stt("dve", t[:, :n], osb, self.gnorm[:, hd:hd + 1], rstd[:, :n], ALU.mult, ALU.mult)
                self.tt("dve", oT[:, hd, c0:c1], t[:, :n], sg[:, c0:c1], ALU.mult)

        nh = getattr(self, 'hmax', 8)
        proj_qz(0)
        proj_gv(0)
        gates(0)
        for hd in range(nh):
            state(hd, proj_qz_groups(hd + 1) if hd + 1 < nh else ())
            output(hd)
            if hd + 1 < nh:
                proj_gv(hd + 1)
                gates(hd + 1)
        units = [[(self.w_ro[m], 8)] for m in range(KC)]
        self.out_proj(li, s, 8, units, lambda k, c0, c1: oT[:, k, c0:c1])

    def build(self):
        nc = self.nc
        self.xT_in = self.dram_in("xT_in", [128, KC, NTOK])
        self.cond_in = self.dram_in("cond_in", [128, 2, KC])
        self.w_ada = self.dram_in("w_ada", [2, 36, 128, 2048])
        self.bada_in = self.dram_in("bada_in", [128, 2, 72])
        self.gpre_in = self.dram_in("gpre_in", [128, 48])
        self.gpost_in = self.dram_in("gpost_in", [128, 48])
        self.w_in = self.dram_in("w_in", [2, 2, NJ, 128, 2048])
        self.w_out = self.dram_in("w_out", [2, 2, 16, 128, 1408])
        self.ropeC_in = self.dram_in("ropeC_in", [128, NTOK])
        self.ropeS_in = self.dram_in("ropeS_in", [128, NTOK])
        self.mask_in = self.dram_in("mask_in", [128, 20 * 128], BF16)
        self.ident_in = self.dram_in("ident_in", [128, 128], BF16)
        self.sinkL_in = self.dram_in("sinkL_in", [1, 256], BF16)
        self.sink_in = self.dram_in("sink_in", [1, 2048])
        self.ckT_in = self.dram_in("ckT_in", [128, 2, 256])
        self.cv_in = self.dram_in("cv_in", [128, 2, 256])
        self.ctxb_in = self.dram_in("ctxb_in", [128, NBLK])
        self.w_qk = self.dram_in("w_qk", [10, 128, 2048])
        self.w_v = self.dram_in("w_v", [128, 2048])
        self.w_o = self.dram_in("w_o", [8, 128, 1024])
        self.w_rf = self.dram_in("w_rf", [8, 2, 128, 2048])
        self.w_rfv = self.dram_in("w_rfv", [8, 128, 1024])
        self.w_ro = self.dram_in("w_ro", [8, 128, 1024])
        self.lbl_in = self.dram_in("lbl_in", [128, 2, 2, 8])
        self.gnorm_in = self.dram_in("gnorm_in", [128, 8])
        self.sinit_in = self.dram_in("sinit_in", [8, 2, 128, 128])
        self.keep_in = self.dram_in("keep_in", [128, 1])
        self.amask_in = self.dram_in("amask_in", [128, 2, 64])
        self.mvec_in = self.dram_in("mvec_in", [128, 2 * NTOK], BF16)
        self.sout = self.dram_out("sout", [5, 2, 8, 128, 128])
        self.yT = self.dram_out("yT", [128, KC, NTOK])
        self.kout = self.dram_out("kout", [128, 2, NTOK])
        self.vout = self.dram_out("vout", [NTOK, 256])
        self.alloc()
        self.dma(self.ctxb[:, :], self.ctxb_in, slot="small")
        self.memset("dve", self.onesD[:, :], 1.0 / D)
        self.memset("dve", self.epsT[:, :], EPS)
        self.dma(self.condT[:, :, :], self.cond_in, slot="small")
        self.dma(self.badaT[:, :, :], self.bada_in, slot="small")
        self.dma(self.gpre[:, :], self.gpre_in, slot="small")
        self.dma(self.gpost[:, :], self.gpost_in, slot="small")
        for kc in range(KC):
            self.dma(self.xT[:, kc, :], self.xT_in[:, kc, :], slot="xin")
        self.act(self.scondb[:, :, 0], self.condT[:, 0, :], AF.Silu)
        self.act(self.scondb[:, :, 1], self.condT[:, 1, :], AF.Silu)
        plan = [(0, 0), (0, 1), (0, 2), (1, 0), (1, 1), (1, 2)][:self.nsub]
        dbg = getattr(self, "dbg", 9)
        for pi, (li, s) in enumerate(plan):
            if dbg < 1:
                break
            self.is_last = (pi + 1 == len(plan)) and dbg >= 9
            if pi + 1 < len(plan) and dbg >= 9:
                li2, s2 = plan[pi + 1]
                self.next_pre = (lambda ti, li2=li2, s2=s2: self.prenorm(li2, s2, ti))
            else:
                self.next_pre = None
            if (li, s) == (0, 0):
                if dbg >= 9:
                    rb = [self.rstd[0], self.rstd[1], self.rs[0]]
                    for ti in range(3):
                        self.prenorm(0, 0, ti, phase="stats", rbuf=rb[ti])
                self.adaln_units(0, 8)
                self.adaln_fin(0, 0, 16)
                if dbg >= 9:
                    for ti in range(3):
                        self.prenorm(0, 0, ti, phase="apply")
                    self.pre_done = True

                def hook0(kind):
                    if kind == "fin":
                        self.adaln_units(0, 36)
                        self.adaln_fin(0, 16, 72)
                    else:
                        self.adaln_units(0, 1)
                self.hook = hook0
            elif (li, s) == (0, 1):
                def hook1a(kind):
                    if kind == "out":
                        self.adaln_units(1, 1)
                self.hook = hook1a
            elif (li, s) == (0, 2):
                def hook1(kind):
                    if kind == "fin":
                        self.adaln_units(1, 36)
                        self.adaln_fin(1, 0, 72)
                    else:
                        self.adaln_units(1, 1)
                self.hook = hook1
            else:
                self.hook = None
            if dbg < 2:
                break
            if (li, s) == (0, 1):
                self.attention(li, s)
                continue
            if (li, s) == (1, 1):
                self.hgrn(li, s)
                continue
            self.maybe_prenorm(li, s)
            if dbg < 3:
                break
            if dbg == 3:
                self.ffn(li, 0, 0)
                break
            if s == 0:
                self.ffn(li, 0, 0)
            elif s == 2:
                self.ffn(li, 1, 2)
            else:
                self.ffn(li, 1, 1)
        if not getattr(self, "y_stored", False):
            for kc in range(KC):
                self.dma(self.yT[:, kc, :], self.xT[:, kc, :], slot="yout")
        self.P.emit_all()
        self.st.close()
        return nc


def core_segments(c):
    if c < 2:
        return [("s", c, k) for k in range(4)] + [("p", 30 + c, 0)]
    return [("p", 5 * (c - 2) + k, 0) for k in range(5)]


def fm(v):
    return np.ascontiguousarray(np.moveaxis(v.reshape(v.shape[:-1] + (8, 128)), -1, 0))


def prep_shared(inp):
    sh = {}
    w_ada = inp["w_ada"]
    wa = w_ada.reshape(2, 8, 128, 36, 2, 128)
    sh["w_ada"] = np.ascontiguousarray(wa.transpose(0, 3, 2, 4, 1, 5)).reshape(2, 36, 128, 2048)
    sh["bada_in"] = np.ascontiguousarray(inp["b_ada"].reshape(2, 72, 128).transpose(2, 0, 1))
    sh["gpre_in"] = np.ascontiguousarray(fm(inp["norm_pre"]).reshape(128, 48))
    sh["gpost_in"] = np.ascontiguousarray(fm(inp["norm_post"]).reshape(128, 48))
    wi = inp["w_ffn_in"].reshape(2, 2, 8, 128, 2, NJ, 128)
    sh["w_in"] = np.ascontiguousarray(wi.transpose(0, 1, 5, 3, 2, 4, 6)).reshape(2, 2, NJ, 128, 2048)
    wo = inp["w_ffn_out"].reshape(2, 2, 2, 11, 128, 8, 128)
    sh["w_out"] = np.ascontiguousarray(wo.transpose(0, 1, 5, 2, 4, 3, 6)).reshape(2, 2, 16, 128, 1408)
    W = inp["w_qkv"][0]

    def partner(d):
        dd = d % 32
        return d + 16 if dd < 16 else d - 16
    dmain = np.arange(64)
    dpart = np.array([partner(d) for d in range(64)])
    units = []
    for i in range(10):
        mains, parts = [], []
        for e in range(2):
            if i < 8:
                pp, r = i // 4, i % 4
                base = ((2 * pp + e) * 4 + r) * 64
            else:
                base = 1024 + (2 * (i - 8) + e) * 64
            mains.append(base + dmain)
            parts.append(base + dpart)
        halves = []
        for cols in (np.concatenate(mains), np.concatenate(parts)):
            halves.append(W[:, cols].reshape(8, 128, 128).transpose(1, 0, 2))
        units.append(np.stack(halves, axis=1).reshape(128, 2048))
    sh["w_qk"] = np.ascontiguousarray(np.stack(units, axis=0))
    sh["w_v"] = np.ascontiguousarray(W[:, 1280:1536].reshape(8, 128, 256).transpose(1, 0, 2)).reshape(128, 2048)
    Wo = inp["w_attn_out"][0]
    rows = np.zeros((8, 128), np.int64)
    for cc in range(8):
        pp, r = cc // 4, cc % 4
        for e in range(2):
            rows[cc, e * 64:(e + 1) * 64] = ((2 * pp + e) * 4 + r) * 64 + dmain
    Wop = Wo[rows.reshape(-1)].reshape(8, 128, 8, 128)
    sh["w_o"] = np.ascontiguousarray(Wop.transpose(2, 1, 0, 3)).reshape(8, 128, 1024)
    Wr = inp["w_rec_in"][0]

    def chunkfm(cols0):
        return Wr[:, cols0:cols0 + 128].reshape(8, 128, 128).transpose(1, 0, 2).reshape(128, 1024)
    wrf = np.zeros((8, 2, 128, 2048), np.float32)
    for hd in range(8):
        wrf[hd, 0, :, :1024] = chunkfm(hd * 128)
        wrf[hd, 0, :, 1024:] = chunkfm(4096 + hd * 128)
        wrf[hd, 1, :, :1024] = chunkfm(2048 + hd * 128)
        wrf[hd, 1, :, 1024:] = chunkfm(3072 + hd * 128)
    sh["w_rf"] = wrf
    sh["w_rfv"] = np.ascontiguousarray(np.stack([chunkfm(1024 + hd * 128) for hd in range(8)], axis=0))
    Wro = inp["w_rec_out"][0].reshape(8, 128, 8, 128)
    sh["w_ro"] = np.ascontiguousarray(Wro.transpose(2, 1, 0, 3)).reshape(8, 128, 1024)
    sh["lbl_in"] = np.ascontiguousarray(inp["rec_lb_logits"].reshape(2, 2, 8, 128).transpose(3, 0, 1, 2))
    sh["gnorm_in"] = np.ascontiguousarray(inp["rec_norm"][0].reshape(8, 128).T)
    am = np.zeros((128, 2, 64), np.float32)
    sidx = (np.arange(128) % 64)[:, None]
    cidx = np.arange(64)[None, :]
    am[:, 0, :] = (sidx <= cidx)
    am[:, 1, :] = (sidx >= cidx)
    sh["amask_in"] = am
    mv = np.ones((128, 2, NTOK), np.float32)
    mv[:, 0, 0::64] = 0.0
    mv[:, 1, 63::64] = 0.0
    sh["mvec_in"] = mv.reshape(128, 2 * NTOK).astype(ml_dtypes.bfloat16)
    sh["ident_in"] = np.eye(128, dtype=np.float32).astype(ml_dtypes.bfloat16)
    sl = np.zeros((1, 256), np.float32)
    sl[0, 64:128] = 1.0
    sl[0, 128:192] = 1.0
    sh["sinkL_in"] = sl.astype(ml_dtypes.bfloat16)
    sh["sink_in"] = np.ascontiguousarray(np.repeat(inp["attn_sink"][0], 128).reshape(1, 2048))
    return sh


def rope_tables(is_sample):
    C = np.ones((128, NTOK), np.float32)
    S = np.zeros((128, NTOK), np.float32)
    if is_sample:
        t = np.arange(1024)
        row = (t // 64).astype(np.float32)
        col = (t % 64).astype(np.float32)
        inv = (np.float32(10000.0) ** (-np.arange(16, dtype=np.float32) / np.float32(16))).astype(np.float32)
        for d in range(64):
            pos = row if d < 32 else col
            dd = d % 32
            ang = (pos * inv[dd % 16]).astype(np.float32)
            cs, sn = np.cos(ang).astype(np.float32), np.sin(ang).astype(np.float32)
            for e in range(2):
                C[e * 64 + d, :1024] = cs
                S[e * 64 + d, :1024] = -sn if dd < 16 else sn
    return C, S


def mask_tables(c):
    NEG = -1e30
    seq_of = [0] * 8 + [1] * 2 if c < 2 else [b // 2 for b in range(10)]
    sample = [c < 2 and b < 8 for b in range(10)]
    m = np.zeros((128, 20, 128), np.float32)
    kk = np.arange(128)[:, None]
    qq = np.arange(128)[None, :]
    for j in range(10):
        for side, nb in ((0, j - 1), (1, j + 1)):
            if nb < 0 or nb > 9:
                continue
            if seq_of[j] != seq_of[nb]:
                m[:, j * 2 + side, :] = NEG
            elif sample[j]:
                valid = (kk >= qq) if side == 0 else (kk <= qq)
                m[:, j * 2 + side, :] = np.where(valid, 0.0, NEG)
    ctxb = np.zeros((128, 10), np.float32)
    for j in range(10):
        if not sample[j]:
            ctxb[:, j] = NEG
    return m.reshape(128, 2560).astype(ml_dtypes.bfloat16), ctxb


def prep_core(inp, c):
    segs = core_segments(c)
    rows = []
    for (kind, b, k) in segs:
        if kind == "s":
            rows.append(inp["x_sample"][b, k * 256:(k + 1) * 256])
        else:
            rows.append(inp["x_prompt"][b])
    x = np.concatenate(rows, axis=0)
    d = {}
    d["xT_in"] = np.ascontiguousarray(x.T.reshape(8, 128, NTOK).transpose(1, 0, 2))
    condA = inp["c"][c] if c < 2 else inp["c_ctx"]
    condB = inp["c_ctx"]
    d["cond_in"] = np.ascontiguousarray(np.stack([condA.reshape(8, 128).T, condB.reshape(8, 128).T], axis=1))
    C, S = rope_tables(c < 2)
    d["ropeC_in"], d["ropeS_in"] = C, S
    d["mask_in"], d["ctxb_in"] = mask_tables(c)
    if c < 2:
        d["sinit_in"] = np.ascontiguousarray(inp["state_s"][c, 0].transpose(1, 0, 2, 3))
        d["keep_in"] = np.ones((128, 1), np.float32)
    else:
        d["sinit_in"] = np.zeros((8, 2, 128, 128), np.float32)
        d["keep_in"] = np.zeros((128, 1), np.float32)
    if c < 2:
        ck = inp["cache_k"][c, 0]
        d["ckT_in"] = np.ascontiguousarray(ck.reshape(256, 2, 2, 64).transpose(2, 3, 1, 0)).reshape(128, 2, 256)
        d["cv_in"] = np.ascontiguousarray(inp["cache_v"][c, 0].reshape(2, 128, 256).transpose(1, 0, 2))
    else:
        d["ckT_in"] = np.zeros((128, 2, 256), np.float32)
        d["cv_in"] = np.zeros((128, 2, 256), np.float32)
    return d


_CACHE = {}


def get_nc(nsub=6):
    if nsub not in _CACHE:
        _CACHE[nsub] = Builder(nsub).build()
    return _CACHE[nsub]


def run_cores(inputs, nsub=6, ncores=NCORES, dbg=9):
    inp = {k: np.asarray(v) for k, v in inputs.items()}
    sh = prep_shared(inp)
    in_maps = []
    for c in range(ncores):
        d = dict(sh)
        d.update(prep_core(inp, c))
        in_maps.append(d)
    bld = Builder(nsub)
    bld.dbg = dbg
    nc = bld.build()
    res = run_bass_kernel_spmd(nc, in_maps, core_ids=list(range(ncores)))
    return res.results


def assemble_x(results):
    yp = np.zeros((32, 256, D), np.float32)
    ys = np.zeros((2, 1024, D), np.float32)
    for c in range(NCORES):
        yT = results[c]["yT"]
        y = yT.transpose(2, 1, 0).reshape(NTOK, D)
        for si, (kind, b, k) in enumerate(core_segments(c)):
            blk = y[si * 256:(si + 1) * 256]
            if kind == "s":
                ys[b, k * 256:(k + 1) * 256] = blk
            else:
                yp[b] = blk
    return yp, ys


def kernel(**inputs):
    results = run_cores(inputs, 6)
    yp, ys = assemble_x(results)
    nk = np.zeros((32, 1, 256, 4, 64), np.float32)
    nv = np.zeros((32, 1, 256, 4, 64), np.float32)
    ns = np.zeros((32, 1, 2, 8, 128, 128), np.float32)
    for c in range(NCORES):
        kout = results[c]["kout"]
        vout = results[c]["vout"]
        sout = results[c]["sout"]
        for si, (kind, b, k) in enumerate(core_segments(c)):
            if kind != "p":
                continue
            kk = kout[:, :, si * 256:(si + 1) * 256].reshape(2, 64, 2, 256)
            nk[b, 0] = kk.transpose(3, 2, 0, 1).reshape(256, 4, 64)
            nv[b, 0] = vout[si * 256:(si + 1) * 256].reshape(256, 4, 64)
            ns[b, 0] = sout[si]
    return yp, ys, nk, nv, ns
```

```python
import numpy as np
import ml_dtypes
import concourse.bass as bass
import concourse.mybir as mybir
from concourse.bass_utils import run_bass_kernel_spmd

F32 = mybir.dt.float32
BF16 = mybir.dt.bfloat16
AF = mybir.ActivationFunctionType
ALU = mybir.AluOpType

NCORES = 8
D = 1024
KC = 8
NTOK = 1280
NBLK = 10
DFF = 2816
NJ = 22
EPS = 1e-6
TILES = [(0, 512), (512, 1024), (1024, 1280)]


class _Op:
    __slots__ = ("eng", "emit", "edeps", "ddeps", "sig", "sigcount", "dma", "waits", "gidx")


class Prog:
    ENGS = ["pe", "act", "dve", "pool", "sp"]

    def __init__(self, nc):
        self.nc = nc
        self.ops = {e: [] for e in self.ENGS}
        self.order = []
        self.recs = {}
        self.dma_cum = {}
        self.elsize = {}

    @staticmethod
    def _region(ap):
        t = ap.tensor
        if type(t).__name__.startswith("DRam"):
            return None
        aps = ap.ap
        pstep, pn = aps[0]
        off = int(ap.offset)
        p0 = off // pstep
        f0 = off % pstep
        f1 = f0 + 1
        for s, n in aps[1:]:
            if s >= 0:
                f1 += (n - 1) * s
            else:
                f0 += (n - 1) * s
        if t.name.startswith("ps"):
            return (t.name, 0, 128, 0, 512)
        return (t.name, p0, p0 + pn, f0, f1)

    def add(self, eng, emit, reads=(), writes=(), dma=None):
        op = _Op()
        op.eng = eng
        op.emit = emit
        op.edeps = {}
        op.ddeps = {}
        op.sig = False
        op.sigcount = 0
        op.dma = None
        op.gidx = len(self.order)
        rregs = [r for r in (self._region(a) for a in reads) if r is not None]
        wregs = [r for r in (self._region(a) for a in writes) if r is not None]

        def dep(rec, raw):
            o = rec[5]
            if o.dma is not None:
                key = o.dma[0]
                op.ddeps[key] = max(op.ddeps.get(key, 0), self.dma_cum[key])
            else:
                op.edeps[o] = op.edeps.get(o, False) or raw

        for (name, p0, p1, f0, f1) in rregs:
            isps = name.startswith("ps")
            for rec in self.recs.get(name, ()):
                if (rec[0] or (isps and rec[5].eng != eng)) and rec[1] < p1 and p0 < rec[2] and rec[3] < f1 and f0 < rec[4]:
                    dep(rec, True)
        for (name, p0, p1, f0, f1) in wregs:
            for rec in self.recs.get(name, ()):
                if rec[1] < p1 and p0 < rec[2] and rec[3] < f1 and f0 < rec[4]:
                    dep(rec, False)
        for (name, p0, p1, f0, f1) in wregs:
            lst = self.recs.setdefault(name, [])
            lst[:] = [r for r in lst if not (p0 <= r[1] and r[2] <= p1 and f0 <= r[3] and r[4] <= f1)]
            lst.append((True, p0, p1, f0, f1, op))
        for (name, p0, p1, f0, f1) in rregs:
            lst = self.recs.setdefault(name, [])
            lst[:] = [r for r in lst if not ((not r[0]) and r[5].eng == eng and r[5].dma is None and dma is None
                                             and p0 <= r[1] and r[2] <= p1 and f0 <= r[3] and r[4] <= f1)]
            lst.append((False, p0, p1, f0, f1, op))
        if dma is not None:
            self.dma_cum[dma] = self.dma_cum.get(dma, 0) + 16
            op.dma = (dma, self.dma_cum[dma])
        self.ops[eng].append(op)
        self.order.append(op)
        return op

    def resolve(self):
        for op in self.order:
            keep = []
            for d, raw in op.edeps.items():
                if d.eng == op.eng and op.eng == "pe" and op.dma is None:
                    continue
                d.sig = True
                keep.append(d)
            op.edeps = keep
        for e in self.ENGS:
            c = 0
            for o in self.ops[e]:
                if o.sig:
                    c += 1
                o.sigcount = c
        self.eng_total = {e: (self.ops[e][-1].sigcount if self.ops[e] else 0) for e in self.ENGS}
        for e in self.ENGS:
            waited = {}
            for o in self.ops[e]:
                w = {}
                for d in o.edeps:
                    k = ("e", d.eng)
                    w[k] = max(w.get(k, 0), d.sigcount)
                for key, v in o.ddeps.items():
                    k = ("d", key)
                    w[k] = max(w.get(k, 0), v)
                o.waits = []
                for k, v in w.items():
                    if waited.get(k, 0) < v:
                        waited[k] = v
                        o.waits.append((k, v))

    def emit_all(self, final_wait_eng="sp"):
        nc = self.nc
        self.resolve()
        import contextlib
        with contextlib.ExitStack() as st:
            esem = {e: st.enter_context(nc.semaphore("sem_" + e)) for e in self.ENGS}
            dsem = {k: st.enter_context(nc.semaphore("dsem_" + k)) for k in self.dma_cum}
            block = st.enter_context(nc.Block())

            def run(e, eng):
                for o in self.ops[e]:
                    for (k, v) in o.waits:
                        sem = esem[k[1]] if k[0] == "e" else dsem[k[1]]
                        eng.wait_ge(sem, v)
                    ins = o.emit(eng)
                    if o.dma is not None:
                        ins.then_inc(dsem[o.dma[0]], 16)
                    elif o.sig:
                        ins.then_inc(esem[e], 1)
                if e == final_wait_eng:
                    for k, v in self.dma_cum.items():
                        eng.wait_ge(dsem[k], v)

            @block.tensor
            def _(eng):
                run("pe", eng)

            @block.scalar
            def _(eng):
                run("act", eng)

            @block.vector
            def _(eng):
                run("dve", eng)

            @block.gpsimd
            def _(eng):
                run("pool", eng)

            @block.sync
            def _(eng):
                run("sp", eng)


class Builder:
    def __init__(self, nsub=6):
        self.nsub = nsub
        self.nc = bass.Bass("TRN2", target_bir_lowering=False)
        self.P = Prog(self.nc)
        self.stg_i = 0
        self.stgb_i = 0
        self.ada_next = {}
        self.hook = None
        self.next_pre = None
        self.pre_done = False
        self.pre_r = {}
        self.ps_rr = 0
        self.tmp_rr = {}

    def mm(self, out, lhsT, rhs, start, stop):
        self.P.add("pe", lambda e: e.matmul(out, lhsT, rhs, start=start, stop=stop), reads=[lhsT, rhs], writes=[out])

    def act(self, out, in_, func, bias=None, scale=1.0):
        reads = [in_]
        kw = {}
        if bias is not None:
            kw["bias"] = bias
            if not isinstance(bias, float):
                reads.append(bias)
        if not isinstance(scale, float):
            reads.append(scale)
        self.P.add("act", lambda e: e.activation(out=out, in_=in_, func=func, scale=scale, **kw), reads=reads, writes=[out])

    def tt(self, eng, out, in0, in1, op):
        self.P.add(eng, lambda e: e.tensor_tensor(out=out, in0=in0, in1=in1, op=op), reads=[in0, in1], writes=[out])

    def stt(self, eng, out, in0, scalar, in1, op0, op1):
        reads = [in0, in1] + ([] if isinstance(scalar, float) else [scalar])
        self.P.add(eng, lambda e: e.scalar_tensor_tensor(out=out, in0=in0, scalar=scalar, in1=in1, op0=op0, op1=op1),
                   reads=reads, writes=[out])

    def ts(self, eng, out, in0, s1, s2, op0, op1=None):
        reads = [in0] + [s for s in (s1, s2) if s is not None and not isinstance(s, float)]
        if op1 is None:
            self.P.add(eng, lambda e: e.tensor_scalar(out=out, in0=in0, scalar1=s1, scalar2=None, op0=op0),
                       reads=reads, writes=[out])
        else:
            self.P.add(eng, lambda e: e.tensor_scalar(out=out, in0=in0, scalar1=s1, scalar2=s2, op0=op0, op1=op1),
                       reads=reads, writes=[out])

    def copy(self, eng, out, in_):
        self.P.add(eng, lambda e: e.tensor_copy(out=out, in_=in_), reads=[in_], writes=[out])

    def recip(self, out, in_):
        self.P.add("dve", lambda e: e.reciprocal(out=out, in_=in_), reads=[in_], writes=[out])

    def memset(self, eng, ap, val):
        self.P.add(eng, lambda e: e.memset(ap, val), reads=[], writes=[ap])

    def dma(self, out, in_, slot, eng="sp"):
        self.P.add(eng, lambda e: e.dma_start(out=out, in_=in_), reads=[in_], writes=[out], dma=slot)

    def alloc(self):
        nc = self.nc
        import contextlib
        self.st = contextlib.ExitStack()
        sb = lambda name, shape, dt: self.st.enter_context(nc.sbuf_tensor(name, shape, dt))
        self.xT = sb("xT", [128, KC, NTOK], F32)
        self.hT = sb("hT", [128, KC, NTOK], BF16)
        self.mid = sb("mid", [128, NJ * NTOK], BF16)
        self.outb = sb("outb", [128, KC * NTOK], F32)
        self.NB = 6
        self.stgb = [sb("stgb%d" % i, [128, 2048], BF16) for i in range(self.NB)]
        self.ps = [self.st.enter_context(nc.psum_tensor("ps%d" % i, [128, 512], F32)) for i in range(8)]
        self.onesD = sb("onesD", [128, 128], BF16)
        self.epsT = sb("epsT", [128, 1], F32)
        self.condT = sb("condT", [128, 2, KC], F32)
        self.scondb = sb("scondb", [128, KC, 2], BF16)
        self.badaT = sb("badaT", [128, 2, 72], F32)
        self.gpre = sb("gpre", [128, 48], F32)
        self.gpost = sb("gpost", [128, 48], F32)
        self.modT = sb("modT", [128, 2, 2, 72], F32)
        self.Av = sb("Av", [128, 2, 2, 24], F32)
        self.Gv = sb("Gv", [128, 2, 2, 24], F32)
        self.rs = [sb("rs%d" % i, [128, 512], F32) for i in range(2)]
        self.rstd = [sb("rstd%d" % i, [128, 512], F32) for i in range(2)]
        self.tmp = [sb("tmp%d" % i, [128, 512], F32) for i in range(3)]
        self.sq = [sb("sq%d" % i, [128, 512], BF16) for i in range(2)]
        self.ctxb = sb("ctxb", [128, NBLK], F32)
        self.lbl = sb("lbl", [128, 2, 2, 8], F32)
        self.lb = sb("lb", [128, 2, 8], F32)
        self.lbs = sb("lbs", [128, 2, 8], F32)
        self.nlbs = sb("nlbs", [128, 2, 8], F32)
        self.gnorm = sb("gnorm", [128, 8], F32)
        self.keep = sb("keep", [128, 1], F32)
        self.amask = sb("amask", [128, 2, 64], F32)
        self.decv = sb("decv", [128, 2, 20], F32)
        self.Sf = sb("Sf", [128, 2, 2, 128], F32)
        self.u2 = [sb("u2_%d" % i, [128, 128], F32) for i in range(3)]
        self.pu_i = 0
        self.onesV = sb("onesV", [128, 128], BF16)
        self.identb = sb("identb", [128, 128], BF16)

    def rr(self, name, lst):
        i = self.tmp_rr.get(name, 0)
        self.tmp_rr[name] = i + 1
        return lst[i % len(lst)]

    def dram_in(self, name, shape, dt=F32):
        return self.nc.dram_tensor(name, list(shape), dt, kind="ExternalInput").ap()

    def dram_out(self, name, shape, dt=F32):
        return self.nc.dram_tensor(name, list(shape), dt, kind="ExternalOutput").ap()

    def load_w(self, src_ap, n):
        s = self.stg[self.stg_i % 2]
        self.stg_i += 1
        self.dma(s[:, 0:n], src_ap, slot="stg%d" % ((self.stg_i - 1) % 2))
        return s

    def load_w_bf(self, src_ap, n):
        k = self.stgb_i % self.NB
        self.stgb_i += 1
        b = self.stgb[k]
        self.dma(b[:, 0:n], src_ap, slot="stgb%d" % k, eng="pool")
        return b

    def adaln_units(self, li, n):
        mp = self.ps[7]
        st = self.ada_next.get(li, 0)
        for u in range(st, min(36, st + n)):
            wb = self.load_w_bf(self.w_ada[li, u], 2048)
            for b in range(2):
                blk = u * 2 + b
                for kc in range(KC):
                    self.mm(mp[:, blk * 2:blk * 2 + 2], wb[:, (b * 8 + kc) * 128:(b * 8 + kc + 1) * 128],
                            self.scondb[:, kc, :], start=(kc == 0), stop=(kc == KC - 1))
        self.ada_next[li] = min(36, st + n)

    def adaln_fin(self, li, b0, b1):
        mp3 = self.ps[7][:, 0:144].rearrange("p (b t) -> p b t", t=2)
        for v in range(2):
            self.tt("dve", self.modT[:, li, v, b0:b1], mp3[:, b0:b1, v], self.badaT[:, li, b0:b1], ALU.add)
        for v in range(2):
            for s in range(3):
                wgt = 1.0 if s == 1 else 0.5
                gp = self.gpre[:, (li * 3 + s) * 8:(li * 3 + s + 1) * 8]
                gq = self.gpost[:, (li * 3 + s) * 8:(li * 3 + s + 1) * 8]
                if b0 <= (s * 3 + 1) * 8 and (s * 3 + 2) * 8 <= b1:
                    sc = self.modT[:, li, v, (s * 3 + 1) * 8:(s * 3 + 2) * 8]
                    self.stt("dve", self.Av[:, li, v, s * 8:(s + 1) * 8], sc, 1.0, gp, ALU.add, ALU.mult)
                if b0 <= (s * 3 + 2) * 8 and (s * 3 + 3) * 8 <= b1:
                    gt = self.modT[:, li, v, (s * 3 + 2) * 8:(s * 3 + 3) * 8]
                    self.stt("dve", self.Gv[:, li, v, s * 8:(s + 1) * 8], gt, wgt, gq, ALU.mult, ALU.mult)

    def rstd_from(self, ss_ps, n, rbuf=None):
        rstd = rbuf if rbuf is not None else self.rr("rstd", self.rstd)
        self.act(rstd[:, :n], ss_ps[:, :n], AF.Ln, bias=self.epsT[:, 0:1])
        self.act(rstd[:, :n], rstd[:, :n], AF.Exp, scale=-0.5)
        return rstd

    def prenorm(self, li, s, ti, phase=None, rbuf=None):
        c0, c1 = TILES[ti]
        n = c1 - c0
        if phase in (None, "stats"):
            ssp = self.ps[2] if ti % 2 == 0 else self.ps[3]
            for kc in range(KC):
                sq = self.rr("sq", self.sq)
                self.act(sq[:, :n], self.xT[:, kc, c0:c1], AF.Square)
                self.mm(ssp[:, :n], self.onesD[:, :], sq[:, :n], start=(kc == 0), stop=(kc == KC - 1))
            self.pre_r[ti] = self.rstd_from(ssp, n, rbuf)
        if phase == "stats":
            return
        rstd = self.pre_r[ti]
        for kc in range(KC):
            t = self.rr("tmp", self.tmp)
            v = 1 if ti == 2 else 0
            self.stt("dve", t[:, :n], self.xT[:, kc, c0:c1], self.Av[:, li, v, s * 8 + kc:s * 8 + kc + 1], rstd[:, :n],
                     ALU.mult, ALU.mult)
            self.act(self.hT[:, kc, c0:c1], t[:, :n], AF.Identity,
                     bias=self.modT[:, li, v, (s * 3) * 8 + kc:(s * 3) * 8 + kc + 1])

    def maybe_prenorm(self, li, s):
        if self.pre_done:
            self.pre_done = False
            return
        for ti in range(3):
            self.prenorm(li, s, ti)

    def postnorm(self, li, s, ti, ssp):
        c0, c1 = TILES[ti]
        n = c1 - c0
        outT = self.outb[:, :].rearrange("p (k t) -> p k t", t=NTOK)
        rstd = self.rstd_from(ssp, n)
        for m in range(KC):
            t = self.rr("tmp", self.tmp)
            v = 1 if ti == 2 else 0
            self.stt("dve", t[:, :n], outT[:, m, c0:c1], self.Gv[:, li, v, s * 8 + m:s * 8 + m + 1], rstd[:, :n],
                     ALU.mult, ALU.mult)
            self.tt("dve" if m >= 4 else "pool", self.xT[:, m, c0:c1], self.xT[:, m, c0:c1], t[:, :n], ALU.add)

    def out_proj(self, li, s, nk, units, rhs_fn):
        outT = self.outb[:, :].rearrange("p (k t) -> p k t", t=NTOK)
        ssb = [self.ps[4], self.ps[5], self.ps[6]]
        it = 0
        for m in range(KC):
            wb = []
            for (ap, nkc) in units[m]:
                b = self.load_w_bf(ap, nkc * 128)
                for k in range(nkc):
                    wb.append(b[:, k * 128:(k + 1) * 128])
            assert len(wb) == nk
            for ti, (c0, c1) in enumerate(TILES):
                n = c1 - c0
                po = self.ps[it % 2]
                it += 1
                for k in range(nk):
                    self.mm(po[:, :n], wb[k], rhs_fn(k, c0, c1), start=(k == 0), stop=(k == nk - 1))
                sq = self.rr("sq", self.sq)
                self.copy("dve", outT[:, m, c0:c1], po[:, :n])
                self.act(sq[:, :n], outT[:, m, c0:c1], AF.Square)
                self.mm(ssb[ti][:, :n], self.onesD[:, :], sq[:, :n], start=(m == 0), stop=(m == KC - 1))
            if self.hook is not None:
                self.hook("out")
        if self.hook is not None:
            self.hook("fin")
        if getattr(self, "dbg", 9) < 5:
            return
        if self.next_pre is not None:
            self.postnorm(li, s, 0, ssb[0])
            self.postnorm(li, s, 1, ssb[1])
            self.next_pre(0)
            self.postnorm(li, s, 2, ssb[2])
            self.next_pre(1)
            self.next_pre(2)
            self.pre_done = True
        else:
            last = getattr(self, "is_last", False)
            for ti in range(3):
                self.postnorm(li, s, ti, ssb[ti])
                if last:
                    c0, c1 = TILES[ti]
                    for kc in range(KC):
                        self.dma(self.yT[:, kc, c0:c1], self.xT[:, kc, c0:c1], slot="yout")
            if last:
                self.y_stored = True

    def ffn(self, li, f, s):
        mid = self.mid[:, :].rearrange("p (j t) -> p j t", t=NTOK)
        cnt = {"it": 0}

        def grp(j, wb, ti):
            c0, c1 = TILES[ti]
            n = c1 - c0
            pa = self.ps[(cnt["it"] % 2) * 2]
            pb = self.ps[(cnt["it"] % 2) * 2 + 1]
            cnt["it"] += 1
            for kc in range(KC):
                self.mm(pa[:, :n], wb[:, kc * 256:kc * 256 + 128], self.hT[:, kc, c0:c1], start=(kc == 0),
                        stop=(kc == KC - 1))
            for kc in range(KC):
                self.mm(pb[:, :n], wb[:, kc * 256 + 128:kc * 256 + 256], self.hT[:, kc, c0:c1], start=(kc == 0),
                        stop=(kc == KC - 1))
            t = self.rr("tmp", self.tmp)
            self.act(t[:, :n], pa[:, :n], AF.Silu)
            self.tt("dve", mid[:, j, c0:c1], t[:, :n], pb[:, :n], ALU.mult)

        JB = 3
        wbs = [self.load_w_bf(self.w_in[li, f, j], 2048) for j in range(JB)]
        for ti in range(3):
            for j in range(JB):
                grp(j, wbs[j], ti)
        for j in range(JB):
            if self.hook is not None:
                self.hook("in")
        for j in range(JB, NJ):
            wb = self.load_w_bf(self.w_in[li, f, j], 2048)
            for ti in range(3):
                grp(j, wb, ti)
            if self.hook is not None:
                self.hook("in")
        if getattr(self, "dbg", 9) < 4:
            return
        units = [[(self.w_out[li, f, m * 2 + h], 11) for h in range(2)] for m in range(KC)]
        self.out_proj(li, s, NJ, units, lambda k, c0, c1: mid[:, k, c0:c1])


    def carve_reset(self):
        self.mid_off = 0
        self.outb_off = 0

    def carve_mid(self, n):
        a = self.mid[:, self.mid_off:self.mid_off + n]
        self.mid_off += n
        assert self.mid_off <= NJ * NTOK
        return a

    def carve_outb(self, n):
        a = self.outb[:, self.outb_off:self.outb_off + n]
        self.outb_off += n
        assert self.outb_off <= KC * NTOK
        return a

    def bcast(self, ap2d, reps):
        (ps, pn), (s1, n1) = ap2d.ap
        return bass.AP(ap2d.tensor, ap2d.offset, [[ps, pn], [0, reps], [s1, n1]])

    def attention(self, li, s):
        self.carve_reset()
        qT = self.carve_mid(8 * NTOK).rearrange("p (c t) -> p c t", t=NTOK)
        kT = self.carve_mid(2 * NTOK).rearrange("p (c t) -> p c t", t=NTOK)
        vaug = self.carve_mid(NBLK * 4 * 128).rearrange("p (b g d) -> p b g d", g=4, d=128)
        ckT = self.carve_mid(2 * 256).rearrange("p (c t) -> p c t", t=256)
        cvaug = self.carve_mid(2 * 4 * 128).rearrange("p (b g d) -> p b g d", g=4, d=128)
        PT = [self.carve_mid(512) for _ in range(6)]
        maskT = self.carve_mid(20 * 128).rearrange("p (m q) -> p m q", q=128)
        sinkrow = self.carve_mid(2048)
        ident = self.carve_mid(128)
        sinkL = self.carve_mid(256)
        ropeC = self.carve_outb(NTOK)
        ropeS = self.carve_outb(NTOK)
        ckf = self.carve_outb(512).rearrange("p (c t) -> p c t", t=256)
        cvf = self.carve_outb(512).rearrange("p (b f) -> p b f", f=256)
        kf = self.carve_outb(2 * NTOK).rearrange("p (c t) -> p c t", t=NTOK)
        vf = self.carve_outb(2 * 256).rearrange("p (b f) -> p b f", f=256)
        sinkf = self.carve_outb(2048)
        oT = self.hT
        self.dma(ropeC, self.ropeC_in, slot="small")
        self.dma(ropeS, self.ropeS_in, slot="small")
        self.dma(maskT, self.mask_in, slot="small")
        self.dma(ident, self.ident_in, slot="small")
        self.dma(sinkL[0:1, :], self.sinkL_in, slot="small")
        self.dma(sinkf[0:1, :], self.sink_in, slot="small")
        self.dma(ckf, self.ckT_in, slot="small")
        self.dma(cvf, self.cv_in, slot="small")
        self.act(sinkrow[0:1, :], sinkf[0:1, :], AF.Exp)
        self.copy("dve", ckT, ckf)
        self.memset("pool", vaug, 1.0)
        self.memset("pool", cvaug, 1.0)
        cv4 = cvf.rearrange("p b (g d) -> p b g d", d=64)
        for b in range(2):
            self.copy("dve", cvaug[:, b, 0::2, 0:64], cv4[:, b, 0::2, :])
            self.copy("dve", cvaug[:, b, 1::2, 64:128], cv4[:, b, 1::2, :])
        self.maybe_prenorm(li, s)
        acnt = {"it": 0}

        def qk_grp(i, wb, ti):
            c0, c1 = TILES[ti]
            n = c1 - c0
            pa = self.ps[(acnt["it"] % 2) * 2]
            pb = self.ps[(acnt["it"] % 2) * 2 + 1]
            acnt["it"] += 1
            for kc in range(KC):
                self.mm(pa[:, :n], wb[:, kc * 128:(kc + 1) * 128], self.hT[:, kc, c0:c1], start=(kc == 0),
                        stop=(kc == KC - 1))
            for kc in range(KC):
                self.mm(pb[:, :n], wb[:, 1024 + kc * 128:1024 + (kc + 1) * 128], self.hT[:, kc, c0:c1],
                        start=(kc == 0), stop=(kc == KC - 1))
            t1 = self.rr("tmp", self.tmp)
            t2 = self.rr("tmp", self.tmp)
            self.tt("dve", t1[:, :n], pa[:, :n], ropeC[:, c0:c1], ALU.mult)
            self.tt("dve", t2[:, :n], pb[:, :n], ropeS[:, c0:c1], ALU.mult)
            dst = qT[:, i, c0:c1] if i < 8 else kT[:, i - 8, c0:c1]
            if i >= 8:
                self.copy("dve", kf[:, i - 8, c0:c1], pa[:, :n])
            self.tt("pool", dst, t1[:, :n], t2[:, :n], ALU.add)

        wq = [self.load_w_bf(self.w_qk[i], 2048) for i in range(3)]
        for ti in range(3):
            for i in range(3):
                qk_grp(i, wq[i], ti)
        for i in range(3, 10):
            wb = self.load_w_bf(self.w_qk[i], 2048)
            for ti in range(3):
                qk_grp(i, wb, ti)
        for cch in range(2):
            self.dma(self.kout[:, cch, :], kf[:, cch, :], slot="kvout")
        wv = self.load_w_bf(self.w_v, 2048)
        for blk in range(NBLK):
            pv = self.ps[4 + blk % 2]
            for kc in range(KC):
                self.mm(pv[:, 0:256], self.hT[:, kc, blk * 128:(blk + 1) * 128], wv[:, kc * 256:(kc + 1) * 256],
                        start=(kc == 0), stop=(kc == KC - 1))
            self.act(vf[:, blk % 2, :], pv[:, 0:256], AF.Copy)
            v4 = vf[:, blk % 2, :].rearrange("p (g d) -> p g d", d=64)
            self.copy("dve", vaug[:, blk, 0::2, 0:64], v4[:, 0::2, :])
            self.copy("dve", vaug[:, blk, 1::2, 64:128], v4[:, 1::2, :])
            self.dma(self.vout[blk * 128:(blk + 1) * 128, :], vf[:, blk % 2, :], slot="kvout")
        sb_i = 0
        ob_i = 0
        for j in range(NBLK):
            for p in range(2):
                for e in range(2):
                    g = 2 * p + e
                    r0, r1 = e * 64, (e + 1) * 64
                    kbs = []
                    if j > 0:
                        kbs.append((kT[r0:r1, p, (j - 1) * 128:j * 128], vaug[:, j - 1, g, :], j * 2, None))
                    kbs.append((kT[r0:r1, p, j * 128:(j + 1) * 128], vaug[:, j, g, :], None, None))
                    if j < NBLK - 1:
                        kbs.append((kT[r0:r1, p, (j + 1) * 128:(j + 2) * 128], vaug[:, j + 1, g, :], j * 2 + 1, None))
                    for cb in range(2):
                        kbs.append((ckT[r0:r1, p, cb * 128:(cb + 1) * 128], cvaug[:, cb, g, :], None,
                                    self.ctxb[:, j:j + 1]))
                    qrhs = qT[r0:r1, p * 4:p * 4 + 4, j * 128:(j + 1) * 128]
                    pts = []
                    for (kap, vap, mi, bias) in kbs:
                        sp = self.ps[sb_i % 6]
                        pt = PT[sb_i % 6]
                        sb_i += 1
                        self.mm(sp[:, :], kap, qrhs, start=True, stop=(mi is None))
                        if mi is not None:
                            self.mm(sp[:, :], ident, self.bcast(maskT[:, mi, :], 4), start=False, stop=True)
                        self.act(pt, sp[:, :], AF.Exp, bias=bias, scale=0.125)
                        pts.append((vap, pt))
                    po = self.ps[6 + ob_i % 2]
                    ob_i += 1
                    for idx, (vap, pt) in enumerate(pts):
                        self.mm(po[:, :], vap, pt, start=(idx == 0), stop=False)
                    self.mm(po[:, :], sinkL[0:1, e * 128:(e + 1) * 128], sinkrow[0:1, g * 512:(g + 1) * 512],
                            start=False, stop=True)
                    rd = self.rr("tmp", self.tmp)
                    d0, d1 = (64, 128) if e == 0 else (0, 64)
                    self.recip(rd[r0:r1, :], po[d0:d1, :])
                    self.tt("dve", oT[r0:r1, p * 4:p * 4 + 4, j * 128:(j + 1) * 128], po[r0:r1, :], rd[r0:r1, :], ALU.mult)
        units = [[(self.w_o[m], 8)] for m in range(KC)]
        self.out_proj(li, s, 8, units, lambda k, c0, c1: oT[:, k, c0:c1])


    def scan(self, out, d0, d1):
        self.P.add("dve", lambda e: e.tensor_tensor_scan(out=out, data0=d0, data1=d1, initial=0.0, op0=ALU.mult, op1=ALU.add),
                   reads=[d0, d1], writes=[out])

    def rev(self, ap2d):
        (ps, pn), (s1, n1) = ap2d.ap
        return bass.AP(ap2d.tensor, ap2d.offset + (n1 - 1) * s1, [[ps, pn], [-s1, n1]])

    def hgrn(self, li, s):
        self.carve_reset()
        vtm = self.carve_mid(NBLK * 128).rearrange("p (b f) -> p b f", f=128)
        oT = self.carve_mid(8 * NTOK).rearrange("p (c t) -> p c t", t=NTOK)
        qd = self.carve_mid(2 * NTOK).rearrange("p (d t) -> p d t", t=NTOK)
        ki = self.carve_mid(2 * NTOK).rearrange("p (d t) -> p d t", t=NTOK)
        kitm = self.carve_mid(2 * NBLK * 128).rearrange("p (d b k) -> p d b k", b=NBLK, k=128)
        AT = self.carve_mid(2 * 640).rearrange("p (d c) -> p d c", c=640)
        Sst = self.carve_mid(5 * 4 * 2 * 128).rearrange("p (g c d v) -> p g c d v", c=4, d=2, v=128)
        ident = self.identb[:, :]
        mvec = self.carve_mid(2 * NTOK).rearrange("p (d t) -> p d t", t=NTOK)
        qs, sg, sf, sb_, lg, kk, bc, eb = [self.carve_outb(NTOK) for _ in range(8)]
        self.dma(ident, self.ident_in, slot="small")
        self.dma(mvec, self.mvec_in, slot="small")
        self.dma(self.lbl[:, :, :, :], self.lbl_in, slot="small")
        self.dma(self.gnorm[:, :], self.gnorm_in, slot="small")
        self.dma(self.keep[:, :], self.keep_in, slot="small")
        self.dma(self.amask[:, :, :], self.amask_in, slot="small")
        self.memset("dve", self.onesV[:, :], 1.0 / 128)
        self.tt("dve", self.lb[:, :, :], self.lbl[:, :, 1, :], self.lbl[:, :, 0, :], ALU.subtract)
        self.act(self.lb[:, :, :], self.lb[:, :, :], AF.Sigmoid)
        self.ts("dve", self.lbs[:, :, :], self.lb[:, :, :], -1.0, 1.0, ALU.mult, ALU.add)
        self.ts("dve", self.nlbs[:, :, :], self.lbs[:, :, :], -1.0, None, ALU.mult)
        self.maybe_prenorm(li, s)
        def proj_one(hd, wsrc, half, dst, fn, bank0):
            for ti, (c0, c1) in enumerate(TILES):
                n = c1 - c0
                pp = self.ps[bank0 + ti % 2]
                for kc in range(KC):
                    self.mm(pp[:, :n], wsrc[:, half * 1024 + kc * 128:half * 1024 + (kc + 1) * 128],
                            self.hT[:, kc, c0:c1], start=(kc == 0), stop=(kc == KC - 1))
                self.act(dst[:, c0:c1], pp[:, :n], fn)

        wcache = {}

        def proj_grp(hd, wsrc, half, dst, fn, ti):
            c0, c1 = TILES[ti]
            n = c1 - c0
            pp = self.ps[ti % 2]
            for kc in range(KC):
                self.mm(pp[:, :n], wsrc[:, half * 1024 + kc * 128:half * 1024 + (kc + 1) * 128],
                        self.hT[:, kc, c0:c1], start=(kc == 0), stop=(kc == KC - 1))
            self.act(dst[:, c0:c1], pp[:, :n], fn)

        def proj_qz_groups(hd):
            w1 = self.load_w_bf(self.w_rf[hd, 0], 2048)
            w2 = self.load_w_bf(self.w_rf[hd, 1], 2048)
            wcache[hd] = w1
            gl = []
            for (wsrc, half, dst, fn) in ((w1, 0, qs, AF.Silu), (w2, 0, sf, AF.Sigmoid), (w2, 1, sb_, AF.Sigmoid)):
                for ti in range(3):
                    gl.append((hd, wsrc, half, dst, fn, ti))
            return gl

        def proj_qz(hd):
            for g_ in proj_qz_groups(hd):
                proj_grp(*g_)

        def proj_gv(hd):
            w1 = wcache[hd]
            proj_one(hd, w1, 1, sg, AF.Silu, 0)
            wv = self.load_w_bf(self.w_rfv[hd], 1024)
            for b0 in range(0, NBLK, 4):
                nb = min(4, NBLK - b0)
                pv = self.ps[2]
                for bb in range(nb):
                    blk = b0 + bb
                    for kc in range(KC):
                        self.mm(pv[:, bb * 128:(bb + 1) * 128], self.hT[:, kc, blk * 128:(blk + 1) * 128],
                                wv[:, kc * 128:(kc + 1) * 128], start=(kc == 0), stop=(kc == KC - 1))
                self.copy("dve", vtm[:, b0:b0 + nb, :], pv[:, 0:nb * 128].rearrange("p (b k) -> p b k", k=128))

        def gates(hd):
            for d in range(2):
                sgm = sf if d == 0 else sb_
                self.act(lg, sgm, AF.Ln, bias=self.lb[:, d, hd:hd + 1], scale=self.lbs[:, d, hd:hd + 1])
                self.ts("dve", kk, sgm, self.nlbs[:, d, hd:hd + 1], self.lbs[:, d, hd:hd + 1], ALU.mult, ALU.add)
                if d == 0:
                    self.scan(bc, mvec[:, 0, :], lg)
                else:
                    self.scan(self.rev(bc), self.rev(mvec[:, 1, :]), self.rev(lg))
                self.act(lg, bc, AF.Exp, scale=-1.0)
                self.act(eb, bc, AF.Exp)
                self.tt("dve", ki[:, d, :], kk, lg, ALU.mult)
                if d == 0:
                    self.copy("dve", self.decv[:, 0, :], eb[:, 63::64])
                else:
                    self.copy("dve", self.decv[:, 1, :], eb[:, 0::64])
                self.stt("dve", qd[:, d, :], qs, 128 ** -0.5, eb, ALU.mult, ALU.mult)
                for b0 in range(0, NBLK, 4):
                    nb = min(4, NBLK - b0)
                    pt = self.ps[4 + (b0 // 4) % 2]
                    for bb in range(nb):
                        blk = b0 + bb
                        self.mm(pt[:, bb * 128:(bb + 1) * 128], ki[:, d, blk * 128:(blk + 1) * 128], ident,
                                start=True, stop=True)
                    self.act(kitm[:, d, b0:b0 + nb, :], pt[:, 0:nb * 128].rearrange("p (b k) -> p b k", k=128), AF.Copy)
                pA, pB = self.ps[6], self.ps[7]
                for blk in range(NBLK):
                    for par in range(2):
                        ch = blk * 2 + par
                        dstp = pA[par * 64:(par + 1) * 64, blk * 64:(blk + 1) * 64] if blk < 8 else \
                            pB[par * 64:(par + 1) * 64, (blk - 8) * 64:(blk - 7) * 64]
                        self.mm(dstp, ki[:, d, ch * 64:(ch + 1) * 64], qd[:, d, ch * 64:(ch + 1) * 64], start=True, stop=True)
                self.tt("dve", AT[:, d, 0:512].rearrange("p (b c) -> p b c", c=64),
                        pA[:, 0:512].rearrange("p (b c) -> p b c", c=64), self.bcast(self.amask[:, d, :], 8), ALU.mult)
                self.tt("dve", AT[:, d, 512:640].rearrange("p (b c) -> p b c", c=64),
                        pB[:, 0:128].rearrange("p (b c) -> p b c", c=64), self.bcast(self.amask[:, d, :], 2), ALU.mult)

        def state(hd, pending=()):
            pending = list(pending)
            orders = []
            for d in range(2):
                seg_order = [0, 1, 2, 3, 4] if d == 0 else [3, 2, 1, 0, 4]
                ch_order = [0, 1, 2, 3] if d == 0 else [3, 2, 1, 0]
                orders.append([(si, seg, ci, ch) for si, seg in enumerate(seg_order) for ci, ch in enumerate(ch_order)])
            cur = [0, 0]
            for step in range(20):
                for d in range(2):
                    si, seg, ci, ch = orders[d][step]
                    S_cur = self.Sf[:, d, cur[d], :]
                    if ci == 0:
                        if si == 0:
                            self.dma(S_cur, self.sinit_in[hd, d], slot="sinit%d" % d)
                        elif seg == 4:
                            self.memset("dve", S_cur, 0.0)
                        else:
                            S_prev = S_cur
                            cur[d] ^= 1
                            S_cur = self.Sf[:, d, cur[d], :]
                            self.ts("dve", S_cur, S_prev, self.keep[:, 0:1], None, ALU.mult)
                    chunk = seg * 4 + ch
                    blk, par = chunk // 2, chunk % 2
                    r0, r1 = par * 64, par * 64 + 64
                    pu = self.ps[3 + self.pu_i % 3]
                    self.pu_i += 1
                    self.mm(pu[:, 0:128], kitm[r0:r1, d, blk, :], vtm[r0:r1, blk, :], start=True, stop=True)
                    u2 = self.rr("u2", self.u2)
                    self.act(u2[:, :], pu[:, 0:128], AF.Copy, scale=self.decv[:, d, chunk:chunk + 1])
                    self.copy("pool" if step % 2 == 0 else "dve", Sst[:, seg, ch, d, :], S_cur)
                    S_nxt = self.Sf[:, d, cur[d] ^ 1, :]
                    self.stt("dve", S_nxt, S_cur, self.decv[:, d, chunk:chunk + 1], u2[:, :], ALU.mult, ALU.add)
                    cur[d] ^= 1
                    if ci == 3:
                        t = self.rr("tmp", self.tmp)
                        self.copy("pool", t[:, 0:128], S_nxt)
                        self.dma(self.sout[seg, d, hd], t[:, 0:128], slot="sout")
                if step % 2 == 1 and pending:
                    proj_grp(*pending.pop(0))
            while pending:
                proj_grp(*pending.pop(0))

        def output(hd):
            for ti, (c0, c1) in enumerate(TILES):
                n = c1 - c0
                po = self.ps[6 + ti % 2]
                for chunk in range(c0 // 64, c1 // 64):
                    seg, ch = chunk // 4, chunk % 4
                    blk, par = chunk // 2, chunk % 2
                    r0, r1 = par * 64, par * 64 + 64
                    col = chunk * 64 - c0
                    dst = po[:, col:col + 64]
                    vl = vtm[r0:r1, blk, :]
                    self.mm(dst, vl, AT[r0:r1, 0, blk * 64:(blk + 1) * 64], start=True, stop=False)
                    self.mm(dst, Sst[:, seg, ch, 0, :], qd[:, 0, chunk * 64:(chunk + 1) * 64], start=False, stop=False)
                    self.mm(dst, vl, AT[r0:r1, 1, blk * 64:(blk + 1) * 64], start=False, stop=False)
                    self.mm(dst, Sst[:, seg, ch, 1, :], qd[:, 1, chunk * 64:(chunk + 1) * 64], start=False, stop=True)
                osb = bc[:, c0:c1]
                self.act(osb, po[:, :n], AF.Copy)
                sq = self.rr("sq", self.sq)
                self.act(sq[:, :n], osb, AF.Square)
                ssp = self.ps[4 + ti % 2]
                self.mm(ssp[:, :n], self.onesV[:, :], sq[:, :n], start=True, stop=True)
                rstd = self.rstd_from(ssp, n)
                t = self.rr("tmp", self.tmp)
                self.stt("dve", t[:, :n], osb, self.gnorm[:, hd:hd + 1], rstd[:, :n], ALU.mult, ALU.mult)
                self.tt("dve", oT[:, hd, c0:c1], t[:, :n], sg[:, c0:c1], ALU.mult)

        nh = getattr(self, 'hmax', 8)
        proj_qz(0)
        proj_gv(0)
        gates(0)
        for hd in range(nh):
            state(hd, proj_qz_groups(hd + 1) if hd + 1 < nh else ())
            output(hd)
            if hd + 1 < nh:
                proj_gv(hd + 1)
                gates(hd + 1)
        units = [[(self.w_ro[m], 8)] for m in range(KC)]
        self.out_proj(li, s, 8, units, lambda k, c0, c1: oT[:, k, c0:c1])

    def build(self):
        nc = self.nc
        self.xT_in = self.dram_in("xT_in", [128, KC, NTOK])
        self.cond_in = self.dram_in("cond_in", [128, 2, KC])
        self.w_ada = self.dram_in("w_ada", [2, 36, 128, 2048])
        self.bada_in = self.dram_in("bada_in", [128, 2, 72])
        self.gpre_in = self.dram_in("gpre_in", [128, 48])
        self.gpost_in = self.dram_in("gpost_in", [128, 48])
        self.w_in = self.dram_in("w_in", [2, 2, NJ, 128, 2048])
        self.w_out = self.dram_in("w_out", [2, 2, 16, 128, 1408])
        self.ropeC_in = self.dram_in("ropeC_in", [128, NTOK])
        self.ropeS_in = self.dram_in("ropeS_in", [128, NTOK])
        self.mask_in = self.dram_in("mask_in", [128, 20 * 128], BF16)
        self.ident_in = self.dram_in("ident_in", [128, 128], BF16)
        self.sinkL_in = self.dram_in("sinkL_in", [1, 256], BF16)
        self.sink_in = self.dram_in("sink_in", [1, 2048])
        self.ckT_in = self.dram_in("ckT_in", [128, 2, 256])
        self.cv_in = self.dram_in("cv_in", [128, 2, 256])
        self.ctxb_in = self.dram_in("ctxb_in", [128, NBLK])
        self.w_qk = self.dram_in("w_qk", [10, 128, 2048])
        self.w_v = self.dram_in("w_v", [128, 2048])
        self.w_o = self.dram_in("w_o", [8, 128, 1024])
        self.w_rf = self.dram_in("w_rf", [8, 2, 128, 2048])
        self.w_rfv = self.dram_in("w_rfv", [8, 128, 1024])
        self.w_ro = self.dram_in("w_ro", [8, 128, 1024])
        self.lbl_in = self.dram_in("lbl_in", [128, 2, 2, 8])
        self.gnorm_in = self.dram_in("gnorm_in", [128, 8])
        self.sinit_in = self.dram_in("sinit_in", [8, 2, 128, 128])
        self.keep_in = self.dram_in("keep_in", [128, 1])
        self.amask_in = self.dram_in("amask_in", [128, 2, 64])
        self.mvec_in = self.dram_in("mvec_in", [128, 2 * NTOK], BF16)
        self.sout = self.dram_out("sout", [5, 2, 8, 128, 128])
        self.yT = self.dram_out("yT", [128, KC, NTOK])
        self.kout = self.dram_out("kout", [128, 2, NTOK])
        self.vout = self.dram_out("vout", [NTOK, 256])
        self.alloc()
        self.dma(self.ctxb[:, :], self.ctxb_in, slot="small")
        self.memset("dve", self.onesD[:, :], 1.0 / D)
        self.memset("dve", self.epsT[:, :], EPS)
        self.dma(self.condT[:, :, :], self.cond_in, slot="small")
        self.dma(self.badaT[:, :, :], self.bada_in, slot="small")
        self.dma(self.gpre[:, :], self.gpre_in, slot="small")
        self.dma(self.gpost[:, :], self.gpost_in, slot="small")
        for kc in range(KC):
            self.dma(self.xT[:, kc, :], self.xT_in[:, kc, :], slot="xin")
        self.act(self.scondb[:, :, 0], self.condT[:, 0, :], AF.Silu)
        self.act(self.scondb[:, :, 1], self.condT[:, 1, :], AF.Silu)
        plan = [(0, 0), (0, 1), (0, 2), (1, 0), (1, 1), (1, 2)][:self.nsub]
        dbg = getattr(self, "dbg", 9)
        for pi, (li, s) in enumerate(plan):
            if dbg < 1:
                break
            self.is_last = (pi + 1 == len(plan)) and dbg >= 9
            if pi + 1 < len(plan) and dbg >= 9:
                li2, s2 = plan[pi + 1]
                self.next_pre = (lambda ti, li2=li2, s2=s2: self.prenorm(li2, s2, ti))
            else:
                self.next_pre = None
            if (li, s) == (0, 0):
                if dbg >= 9:
                    rb = [self.rstd[0], self.rstd[1], self.rs[0]]
                    for ti in range(3):
                        self.prenorm(0, 0, ti, phase="stats", rbuf=rb[ti])
                self.adaln_units(0, 8)
                self.adaln_fin(0, 0, 16)
                if dbg >= 9:
                    for ti in range(3):
                        self.prenorm(0, 0, ti, phase="apply")
                    self.pre_done = True

                def hook0(kind):
                    if kind == "fin":
                        self.adaln_units(0, 36)
                        self.adaln_fin(0, 16, 72)
                    else:
                        self.adaln_units(0, 1)
                self.hook = hook0
            elif (li, s) == (0, 1):
                def hook1a(kind):
                    if kind == "out":
                        self.adaln_units(1, 1)
                self.hook = hook1a
            elif (li, s) == (0, 2):
                def hook1(kind):
                    if kind == "fin":
                        self.adaln_units(1, 36)
                        self.adaln_fin(1, 0, 72)
                    else:
                        self.adaln_units(1, 1)
                self.hook = hook1
            else:
                self.hook = None
            if dbg < 2:
                break
            if (li, s) == (0, 1):
                self.attention(li, s)
                continue
            if (li, s) == (1, 1):
                self.hgrn(li, s)
                continue
            self.maybe_prenorm(li, s)
            if dbg < 3:
                break
            if dbg == 3:
                self.ffn(li, 0, 0)
                break
            if s == 0:
                self.ffn(li, 0, 0)
            elif s == 2:
                self.ffn(li, 1, 2)
            else:
                self.ffn(li, 1, 1)
        if not getattr(self, "y_stored", False):
            for kc in range(KC):
                self.dma(self.yT[:, kc, :], self.xT[:, kc, :], slot="yout")
        self.P.emit_all()
        self.st.close()
        return nc


def core_segments(c):
    if c < 2:
        return [("s", c, k) for k in range(4)] + [("p", 30 + c, 0)]
    return [("p", 5 * (c - 2) + k, 0) for k in range(5)]


def fm(v):
    return np.ascontiguousarray(np.moveaxis(v.reshape(v.shape[:-1] + (8, 128)), -1, 0))


def prep_shared(inp):
    sh = {}
    w_ada = inp["w_ada"]
    wa = w_ada.reshape(2, 8, 128, 36, 2, 128)
    sh["w_ada"] = np.ascontiguousarray(wa.transpose(0, 3, 2, 4, 1, 5)).reshape(2, 36, 128, 2048)
    sh["bada_in"] = np.ascontiguousarray(inp["b_ada"].reshape(2, 72, 128).transpose(2, 0, 1))
    sh["gpre_in"] = np.ascontiguousarray(fm(inp["norm_pre"]).reshape(128, 48))
    sh["gpost_in"] = np.ascontiguousarray(fm(inp["norm_post"]).reshape(128, 48))
    wi = inp["w_ffn_in"].reshape(2, 2, 8, 128, 2, NJ, 128)
    sh["w_in"] = np.ascontiguousarray(wi.transpose(0, 1, 5, 3, 2, 4, 6)).reshape(2, 2, NJ, 128, 2048)
    wo = inp["w_ffn_out"].reshape(2, 2, 2, 11, 128, 8, 128)
    sh["w_out"] = np.ascontiguousarray(wo.transpose(0, 1, 5, 2, 4, 3, 6)).reshape(2, 2, 16, 128, 1408)
    W = inp["w_qkv"][0]

    def partner(d):
        dd = d % 32
        return d + 16 if dd < 16 else d - 16
    dmain = np.arange(64)
    dpart = np.array([partner(d) for d in range(64)])
    units = []
    for i in range(10):
        mains, parts = [], []
        for e in range(2):
            if i < 8:
                pp, r = i // 4, i % 4
                base = ((2 * pp + e) * 4 + r) * 64
            else:
                base = 1024 + (2 * (i - 8) + e) * 64
            mains.append(base + dmain)
            parts.append(base + dpart)
        halves = []
        for cols in (np.concatenate(mains), np.concatenate(parts)):
            halves.append(W[:, cols].reshape(8, 128, 128).transpose(1, 0, 2))
        units.append(np.stack(halves, axis=1).reshape(128, 2048))
    sh["w_qk"] = np.ascontiguousarray(np.stack(units, axis=0))
    sh["w_v"] = np.ascontiguousarray(W[:, 1280:1536].reshape(8, 128, 256).transpose(1, 0, 2)).reshape(128, 2048)
    Wo = inp["w_attn_out"][0]
    rows = np.zeros((8, 128), np.int64)
    for cc in range(8):
        pp, r = cc // 4, cc % 4
        for e in range(2):
            rows[cc, e * 64:(e + 1) * 64] = ((2 * pp + e) * 4 + r) * 64 + dmain
    Wop = Wo[rows.reshape(-1)].reshape(8, 128, 8, 128)
    sh["w_o"] = np.ascontiguousarray(Wop.transpose(2, 1, 0, 3)).reshape(8, 128, 1024)
    Wr = inp["w_rec_in"][0]

    def chunkfm(cols0):
        return Wr[:, cols0:cols0 + 128].reshape(8, 128, 128).transpose(1, 0, 2).reshape(128, 1024)
    wrf = np.zeros((8, 2, 128, 2048), np.float32)
    for hd in range(8):
        wrf[hd, 0, :, :1024] = chunkfm(hd * 128)
        wrf[hd, 0, :, 1024:] = chunkfm(4096 + hd * 128)
        wrf[hd, 1, :, :1024] = chunkfm(2048 + hd * 128)
        wrf[hd, 1, :, 1024:] = chunkfm(3072 + hd * 128)
    sh["w_rf"] = wrf
    sh["w_rfv"] = np.ascontiguousarray(np.stack([chunkfm(1024 + hd * 128) for hd in range(8)], axis=0))
    Wro = inp["w_rec_out"][0].reshape(8, 128, 8, 128)
    sh["w_ro"] = np.ascontiguousarray(Wro.transpose(2, 1, 0, 3)).reshape(8, 128, 1024)
    sh["lbl_in"] = np.ascontiguousarray(inp["rec_lb_logits"].reshape(2, 2, 8, 128).transpose(3, 0, 1, 2))
    sh["gnorm_in"] = np.ascontiguousarray(inp["rec_norm"][0].reshape(8, 128).T)
    am = np.zeros((128, 2, 64), np.float32)
    sidx = (np.arange(128) % 64)[:, None]
    cidx = np.arange(64)[None, :]
    am[:, 0, :] = (sidx <= cidx)
    am[:, 1, :] = (sidx >= cidx)
    sh["amask_in"] = am
    mv = np.ones((128, 2, NTOK), np.float32)
    mv[:, 0, 0::64] = 0.0
    mv[:, 1, 63::64] = 0.0
    sh["mvec_in"] = mv.reshape(128, 2 * NTOK).astype(ml_dtypes.bfloat16)
    sh["ident_in"] = np.eye(128, dtype=np.float32).astype(ml_dtypes.bfloat16)
    sl = np.zeros((1, 256), np.float32)
    sl[0, 64:128] = 1.0
    sl[0, 128:192] = 1.0
    sh["sinkL_in"] = sl.astype(ml_dtypes.bfloat16)
    sh["sink_in"] = np.ascontiguousarray(np.repeat(inp["attn_sink"][0], 128).reshape(1, 2048))
    return sh


def rope_tables(is_sample):
    C = np.ones((128, NTOK), np.float32)
    S = np.zeros((128, NTOK), np.float32)
    if is_sample:
        t = np.arange(1024)
        row = (t // 64).astype(np.float32)
        col = (t % 64).astype(np.float32)
        inv = (np.float32(10000.0) ** (-np.arange(16, dtype=np.float32) / np.float32(16))).astype(np.float32)
        for d in range(64):
            pos = row if d < 32 else col
            dd = d % 32
            ang = (pos * inv[dd % 16]).astype(np.float32)
            cs, sn = np.cos(ang).astype(np.float32), np.sin(ang).astype(np.float32)
            for e in range(2):
                C[e * 64 + d, :1024] = cs
                S[e * 64 + d, :1024] = -sn if dd < 16 else sn
    return C, S


def mask_tables(c):
    NEG = -1e30
    seq_of = [0] * 8 + [1] * 2 if c < 2 else [b // 2 for b in range(10)]
    sample = [c < 2 and b < 8 for b in range(10)]
    m = np.zeros((128, 20, 128), np.float32)
    kk = np.arange(128)[:, None]
    qq = np.arange(128)[None, :]
    for j in range(10):
        for side, nb in ((0, j - 1), (1, j + 1)):
            if nb < 0 or nb > 9:
                continue
            if seq_of[j] != seq_of[nb]:
                m[:, j * 2 + side, :] = NEG
            elif sample[j]:
                valid = (kk >= qq) if side == 0 else (kk <= qq)
                m[:, j * 2 + side, :] = np.where(valid, 0.0, NEG)
    ctxb = np.zeros((128, 10), np.float32)
    for j in range(10):
        if not sample[j]:
            ctxb[:, j] = NEG
    return m.reshape(128, 2560).astype(ml_dtypes.bfloat16), ctxb


def prep_core(inp, c):
    segs = core_segments(c)
    rows = []
    for (kind, b, k) in segs:
        if kind == "s":
            rows.append(inp["x_sample"][b, k * 256:(k + 1) * 256])
        else:
            rows.append(inp["x_prompt"][b])
    x = np.concatenate(rows, axis=0)
    d = {}
    d["xT_in"] = np.ascontiguousarray(x.T.reshape(8, 128, NTOK).transpose(1, 0, 2))
    condA = inp["c"][c] if c < 2 else inp["c_ctx"]
    condB = inp["c_ctx"]
    d["cond_in"] = np.ascontiguousarray(np.stack([condA.reshape(8, 128).T, condB.reshape(8, 128).T], axis=1))
    C, S = rope_tables(c < 2)
    d["ropeC_in"], d["ropeS_in"] = C, S
    d["mask_in"], d["ctxb_in"] = mask_tables(c)
    if c < 2:
        d["sinit_in"] = np.ascontiguousarray(inp["state_s"][c, 0].transpose(1, 0, 2, 3))
        d["keep_in"] = np.ones((128, 1), np.float32)
    else:
        d["sinit_in"] = np.zeros((8, 2, 128, 128), np.float32)
        d["keep_in"] = np.zeros((128, 1), np.float32)
    if c < 2:
        ck = inp["cache_k"][c, 0]
        d["ckT_in"] = np.ascontiguousarray(ck.reshape(256, 2, 2, 64).transpose(2, 3, 1, 0)).reshape(128, 2, 256)
        d["cv_in"] = np.ascontiguousarray(inp["cache_v"][c, 0].reshape(2, 128, 256).transpose(1, 0, 2))
    else:
        d["ckT_in"] = np.zeros((128, 2, 256), np.float32)
        d["cv_in"] = np.zeros((128, 2, 256), np.float32)
    return d


_CACHE = {}


def get_nc(nsub=6):
    if nsub not in _CACHE:
        _CACHE[nsub] = Builder(nsub).build()
    return _CACHE[nsub]


def run_cores(inputs, nsub=6, ncores=NCORES, dbg=9):
    inp = {k: np.asarray(v) for k, v in inputs.items()}
    sh = prep_shared(inp)
    in_maps = []
    for c in range(ncores):
        d = dict(sh)
        d.update(prep_core(inp, c))
        in_maps.append(d)
    bld = Builder(nsub)
    bld.dbg = dbg
    nc = bld.build()
    res = run_bass_kernel_spmd(nc, in_maps, core_ids=list(range(ncores)))
    return res.results


def assemble_x(results):
    yp = np.zeros((32, 256, D), np.float32)
    ys = np.zeros((2, 1024, D), np.float32)
    for c in range(NCORES):
        yT = results[c]["yT"]
        y = yT.transpose(2, 1, 0).reshape(NTOK, D)
        for si, (kind, b, k) in enumerate(core_segments(c)):
            blk = y[si * 256:(si + 1) * 256]
            if kind == "s":
                ys[b, k * 256:(k + 1) * 256] = blk
            else:
                yp[b] = blk
    return yp, ys


def kernel(**inputs):
    results = run_cores(inputs, 6)
    yp, ys = assemble_x(results)
    nk = np.zeros((32, 1, 256, 4, 64), np.float32)
    nv = np.zeros((32, 1, 256, 4, 64), np.float32)
    ns = np.zeros((32, 1, 2, 8, 128, 128), np.float32)
    for c in range(NCORES):
        kout = results[c]["kout"]
        vout = results[c]["vout"]
        sout = results[c]["sout"]
        for si, (kind, b, k) in enumerate(core_segments(c)):
            if kind != "p":
                continue
            kk = kout[:, :, si * 256:(si + 1) * 256].reshape(2, 64, 2, 256)
            nk[b, 0] = kk.transpose(3, 2, 0, 1).reshape(256, 4, 64)
            nv[b, 0] = vout[si * 256:(si + 1) * 256].reshape(256, 4, 64)
            ns[b, 0] = sout[si]
    return yp, ys, nk, nv, ns
```
